# Optimizing a Trainium2 kernel written in Bass

```python
import jax, jax.numpy as jnp
from jax import lax
import numpy as np

D_MODEL = 1024
BATCH = 8
SEQ = 4096
DEPTH = 2

GRID_W = 64
CTX_LEN = 256
D_FF = 2816
FFN_RES = 0.5
N_MOD = 9
EPS = 1e-6
CONV_CH = 512
CONV_GROUPS = 8
CONV_WIDTH = 31
NA_HEADS = 8
NA_HEAD_DIM = 64
NA_WIDTH = NA_HEADS * NA_HEAD_DIM
NA_WIN_R = 8
NA_WIN_C = 16
EVEN_IN = 2 * CONV_CH + 3 * NA_WIDTH
MIX_WIDTH = CONV_CH + NA_WIDTH
SC_WIDTH = D_MODEL
SC_CONV = 3
N_EVEN = (DEPTH + 1) // 2
N_ODD = DEPTH // 2

kernel_name = "hybrid_conformer_natten_shortconv_dit"


def rms_norm(x, g):
    xf = x.astype(jnp.float32)
    y = xf * lax.rsqrt(jnp.mean(xf * xf, axis=-1, keepdims=True) + EPS)
    return (y * g.astype(jnp.float32)).astype(x.dtype)


def ada_mod(cond, w, b):
    m = jax.nn.silu(cond) @ w + b
    return m.reshape(cond.shape[0], N_MOD, D_MODEL)


def pre(x, m, k, g):
    return rms_norm(x, g) * (1 + m[:, 3 * k + 1, None]) + m[:, 3 * k, None]


def post(y, m, k, g):
    return m[:, 3 * k + 2, None] * rms_norm(y, g)


def swiglu(h, w_gu, w_down):
    gt, up = jnp.split(h @ w_gu, 2, axis=-1)
    return (jax.nn.silu(gt) * up) @ w_down


def dw_conv(x, w):
    k = w.shape[0]
    return lax.conv_general_dilated(
        x, w[:, None, :].astype(x.dtype), window_strides=(1,),
        padding=[(k // 2, k // 2)], dimension_numbers=("NWC", "WIO", "NWC"),
        feature_group_count=x.shape[-1])


def conformer_conv(u, dw_w, dw_b, ln_g, ln_b):
    a, gt = jnp.split(u, 2, axis=-1)
    v = dw_conv(a * jax.nn.sigmoid(gt), dw_w) + dw_b
    vg = v.reshape(v.shape[:-1] + (CONV_GROUPS, CONV_CH // CONV_GROUPS)).astype(jnp.float32)
    mu = jnp.mean(vg, axis=-1, keepdims=True)
    var = jnp.mean(jnp.square(vg - mu), axis=-1, keepdims=True)
    vn = ((vg - mu) * lax.rsqrt(var + EPS)).reshape(v.shape)
    vn = vn * ln_g.astype(jnp.float32) + ln_b.astype(jnp.float32)
    return jax.nn.silu(vn).astype(u.dtype)


def ctx_attention(q, k, v):
    s = jnp.einsum('bqhd,bkhd->bhqk', q, k).astype(jnp.float32) * (NA_HEAD_DIM ** -0.5)
    p = jax.nn.softmax(s, axis=-1).astype(v.dtype)
    return jnp.einsum('bhqk,bkhd->bqhd', p, v)


def neighbourhood_attention(q, k, v, k_ctx, v_ctx, rpb):
    bsz, s, h, dh = q.shape
    rows = s // GRID_W
    kr = min(NA_WIN_R, rows)
    kc = NA_WIN_C
    qg = q.reshape(bsz, rows, GRID_W, h, dh)
    kg = k.reshape(bsz, rows, GRID_W, h, dh)
    vg = v.reshape(bsz, rows, GRID_W, h, dh)
    row_start = jnp.clip(jnp.arange(rows) - kr // 2, 0, rows - kr)
    cols = jnp.arange(GRID_W)
    col_start = jnp.clip(cols - kc // 2, 0, GRID_W - kc)
    col_idx = col_start[:, None] + jnp.arange(kc)[None, :]
    col_bias_idx = col_idx - cols[:, None] + (NA_WIN_C - 1)
    scale = NA_HEAD_DIM ** -0.5

    def one_row(r):
        q_r = lax.dynamic_index_in_dim(qg, r, axis=1, keepdims=False)
        rs = row_start[r]
        k_rows = lax.dynamic_slice_in_dim(kg, rs, kr, axis=1)
        v_rows = lax.dynamic_slice_in_dim(vg, rs, kr, axis=1)
        k_win = k_rows[:, :, col_idx]
        v_win = v_rows[:, :, col_idx]
        row_bias_idx = rs + jnp.arange(kr) - r + (NA_WIN_R - 1)
        bias = rpb[:, row_bias_idx][:, :, col_bias_idx]
        bias = jnp.transpose(bias, (0, 2, 1, 3)).astype(jnp.float32)
        s_win = jnp.einsum('bwhd,brwjhd->bhwrj', q_r, k_win).astype(jnp.float32) * scale + bias
        s_ctx = jnp.einsum('bwhd,bchd->bhwc', q_r, k_ctx).astype(jnp.float32) * scale
        logits = jnp.concatenate([s_win.reshape(bsz, h, GRID_W, kr * kc), s_ctx], axis=-1)
        p = jax.nn.softmax(logits, axis=-1).astype(v.dtype)
        p_win = p[..., :kr * kc].reshape(bsz, h, GRID_W, kr, kc)
        p_ctx = p[..., kr * kc:]
        return (jnp.einsum('bhwrj,brwjhd->bwhd', p_win, v_win)
                + jnp.einsum('bhwc,bchd->bwhd', p_ctx, v_ctx))

    out = lax.map(one_row, jnp.arange(rows))
    return jnp.moveaxis(out, 0, 1).reshape(bsz, s, h * dh)


def even_mixer(h_lat, h_ctx, w_in, w_out, dw_w, dw_b, ln_g, ln_b, rpb, ctx_out):
    bsz, s, _ = h_lat.shape
    lc = h_ctx.shape[1]
    u = h_lat @ w_in
    y_a = conformer_conv(u[..., :2 * CONV_CH], dw_w, dw_b, ln_g, ln_b)
    q, k, v = jnp.split(u[..., 2 * CONV_CH:], 3, axis=-1)
    q = q.reshape(bsz, s, NA_HEADS, NA_HEAD_DIM)
    k = k.reshape(bsz, s, NA_HEADS, NA_HEAD_DIM)
    v = v.reshape(bsz, s, NA_HEADS, NA_HEAD_DIM)
    y_ctx = None
    if ctx_out:
        u_c = h_ctx @ w_in
        y_a_c = conformer_conv(u_c[..., :2 * CONV_CH], dw_w, dw_b, ln_g, ln_b)
        q_c, k_c, v_c = jnp.split(u_c[..., 2 * CONV_CH:], 3, axis=-1)
        q_c = q_c.reshape(bsz, lc, NA_HEADS, NA_HEAD_DIM)
        k_c = k_c.reshape(bsz, lc, NA_HEADS, NA_HEAD_DIM)
        v_c = v_c.reshape(bsz, lc, NA_HEADS, NA_HEAD_DIM)
        y_b_c = ctx_attention(q_c, k_c, v_c).reshape(bsz, lc, NA_WIDTH)
        y_ctx = jnp.concatenate([y_a_c, y_b_c], axis=-1) @ w_out
    else:
        k_c, v_c = jnp.split(h_ctx @ w_in[:, 2 * CONV_CH + NA_WIDTH:], 2, axis=-1)
        k_c = k_c.reshape(bsz, lc, NA_HEADS, NA_HEAD_DIM)
        v_c = v_c.reshape(bsz, lc, NA_HEADS, NA_HEAD_DIM)
    y_b = neighbourhood_attention(q, k, v, k_c, v_c, rpb)
    y_lat = jnp.concatenate([y_a, y_b], axis=-1) @ w_out
    return y_lat, y_ctx


def short_conv_mixer(h, w_in, w_conv, w_out):
    b_gate, c_gate, xv = jnp.split(h @ w_in, 3, axis=-1)
    return (b_gate * dw_conv(c_gate * xv, w_conv)) @ w_out


def setup_inputs(seed: int = 0) -> dict:
    key = jax.random.key(seed)
    ks = jax.random.split(key, 24)
    nrm = lambda k, shp, s: jax.random.normal(k, shp, jnp.float32) * s
    return {
        "x": nrm(ks[0], (BATCH, SEQ, D_MODEL), 1.0),
        "c": nrm(ks[1], (BATCH, D_MODEL), 1.0),
        "ctx": nrm(ks[2], (BATCH, CTX_LEN, D_MODEL), 1.0),
        "c_ctx": nrm(ks[3], (D_MODEL,), 1.0),
        "w_mod": nrm(ks[4], (DEPTH, D_MODEL, N_MOD * D_MODEL), 0.5 * D_MODEL ** -0.5),
        "b_mod": nrm(ks[5], (DEPTH, N_MOD * D_MODEL), 0.01),
        "norm_g": 1.0 + nrm(ks[6], (DEPTH, 6, D_MODEL), 0.05),
        "ff1_w_gu": nrm(ks[7], (DEPTH, D_MODEL, 2 * D_FF), D_MODEL ** -0.5),
        "ff1_w_down": nrm(ks[8], (DEPTH, D_FF, D_MODEL), D_FF ** -0.5),
        "ff2_w_gu": nrm(ks[9], (DEPTH, D_MODEL, 2 * D_FF), D_MODEL ** -0.5),
        "ff2_w_down": nrm(ks[10], (DEPTH, D_FF, D_MODEL), D_FF ** -0.5),
        "ev_w_in": nrm(ks[11], (N_EVEN, D_MODEL, EVEN_IN), D_MODEL ** -0.5),
        "ev_w_out": nrm(ks[12], (N_EVEN, MIX_WIDTH, D_MODEL), MIX_WIDTH ** -0.5),
        "ev_dw_w": nrm(ks[13], (N_EVEN, CONV_WIDTH, CONV_CH), CONV_WIDTH ** -0.5),
        "ev_dw_b": nrm(ks[14], (N_EVEN, CONV_CH), 0.01),
        "ev_ln_g": 1.0 + nrm(ks[15], (N_EVEN, CONV_CH), 0.05),
        "ev_ln_b": nrm(ks[16], (N_EVEN, CONV_CH), 0.01),
        "ev_rpb": nrm(ks[17], (N_EVEN, NA_HEADS, 2 * NA_WIN_R - 1, 2 * NA_WIN_C - 1), 0.1),
        "od_w_in": nrm(ks[18], (N_ODD, D_MODEL, 3 * SC_WIDTH), D_MODEL ** -0.5),
        "od_conv_w": nrm(ks[19], (N_ODD, SC_CONV, SC_WIDTH), SC_CONV ** -0.5),
        "od_w_out": nrm(ks[20], (N_ODD, SC_WIDTH, D_MODEL), SC_WIDTH ** -0.5),
    }


def reference(x, c, ctx, c_ctx, w_mod, b_mod, norm_g, ff1_w_gu, ff1_w_down, ff2_w_gu, ff2_w_down,
              ev_w_in, ev_w_out, ev_dw_w, ev_dw_b, ev_ln_g, ev_ln_b, ev_rpb,
              od_w_in, od_conv_w, od_w_out):
    x_lat, x_ctx = x, ctx
    for i in range(DEPTH):
        ctx_in = any(j % 2 == 0 for j in range(i, DEPTH))
        ctx_out = any(j % 2 == 0 for j in range(i + 1, DEPTH))
        g = norm_g[i]
        m_lat = ada_mod(c, w_mod[i], b_mod[i])
        m_ctx = ada_mod(c_ctx[None], w_mod[i], b_mod[i])

        x_lat = x_lat + FFN_RES * post(swiglu(pre(x_lat, m_lat, 0, g[0]), ff1_w_gu[i], ff1_w_down[i]), m_lat, 0, g[1])
        if ctx_in:
            x_ctx = x_ctx + FFN_RES * post(swiglu(pre(x_ctx, m_ctx, 0, g[0]), ff1_w_gu[i], ff1_w_down[i]), m_ctx, 0, g[1])

        if i % 2 == 0:
            e = i // 2
            h_l = pre(x_lat, m_lat, 1, g[2])
            h_c = pre(x_ctx, m_ctx, 1, g[2])
            y_l, y_c = even_mixer(h_l, h_c, ev_w_in[e], ev_w_out[e], ev_dw_w[e], ev_dw_b[e],
                                  ev_ln_g[e], ev_ln_b[e], ev_rpb[e], ctx_out)
            x_lat = x_lat + post(y_l, m_lat, 1, g[3])
            if ctx_out:
                x_ctx = x_ctx + post(y_c, m_ctx, 1, g[3])
        else:
            o = i // 2
            x_lat = x_lat + post(short_conv_mixer(pre(x_lat, m_lat, 1, g[2]), od_w_in[o], od_conv_w[o], od_w_out[o]), m_lat, 1, g[3])
            if ctx_out:
                x_ctx = x_ctx + post(short_conv_mixer(pre(x_ctx, m_ctx, 1, g[2]), od_w_in[o], od_conv_w[o], od_w_out[o]), m_ctx, 1, g[3])

        x_lat = x_lat + FFN_RES * post(swiglu(pre(x_lat, m_lat, 2, g[4]), ff2_w_gu[i], ff2_w_down[i]), m_lat, 2, g[5])
        if ctx_out:
            x_ctx = x_ctx + FFN_RES * post(swiglu(pre(x_ctx, m_ctx, 2, g[4]), ff2_w_gu[i], ff2_w_down[i]), m_ctx, 2, g[5])
    return x_lat
```

```python
import contextlib
import numpy as np
import concourse.bass as bass
import concourse.mybir as mybir
from concourse.bass_utils import run_bass_kernel_spmd

F32 = mybir.dt.float32
BF16 = mybir.dt.bfloat16
AF = mybir.ActivationFunctionType
ALU = mybir.AluOpType

D = 1024
SEQ = 4096
CTX = 256
DFF = 2816
NFC = 22
EPS = 1e-6
NEG = -30000.0
ENGS = ("pe", "act", "dve", "pool", "sp")


class Buf:
    __slots__ = ("name", "lw", "rd")

    def __init__(self, name=""):
        self.name = name
        self.lw = None
        self.rd = {}


class Op:
    __slots__ = ("eng", "fn", "deps", "sig", "tok", "dma", "key", "phase")


class Prog:
    def __init__(self):
        self.by_eng = {e: [] for e in ENGS}
        self.ops = []
        self.last = {e: None for e in ENGS}
        self.dmas = []
        self.phase = 0

    def add(self, eng, fn, reads=(), writes=(), key=None):
        op = Op()
        op.eng, op.fn, op.dma, op.key, op.sig, op.tok = eng, fn, key is not None, key, False, None
        op.phase = self.phase
        deps = {}

        def dep(o):
            if o is None or o is op:
                return
            if (not o.dma) and (not op.dma) and o.eng == "pe" and eng == "pe":
                return
            deps[id(o)] = o

        for b in reads:
            dep(b.lw)
        for b in writes:
            dep(b.lw)
            for r in b.rd.values():
                dep(r)
        rk = id(op) if op.dma else eng
        for b in reads:
            b.rd[rk] = op
        for b in writes:
            b.lw = op
            b.rd = {}
        op.deps = list(deps.values())
        self.ops.append(op)
        self.by_eng[eng].append(op)
        if op.dma:
            self.dmas.append(op)
        else:
            self.last[eng] = op
        return op

    def barrier(self, engines=ENGS):
        lastd = {}
        for o in self.dmas:
            lastd[id(o.key)] = o
        deps = [o for o in self.last.values() if o is not None] + list(lastd.values())
        self.dmas = []
        if len(engines) == len(ENGS):
            self.phase += 1
        for e in engines:
            op = Op()
            op.eng, op.fn, op.dma, op.key, op.sig, op.tok = e, None, False, None, False, None
            op.phase = self.phase
            op.deps = [d for d in deps if not (d.eng == "pe" and e == "pe" and not d.dma)]
            self.ops.append(op)
            self.by_eng[e].append(op)

    def emit(self, nc, stack):
        EPOCH = 1000
        for op in self.ops:
            for d in op.deps:
                d.sig = True
        slots, slotd, keymap = [], {}, {}
        for op in self.ops:
            if op.fn is None or not op.dma:
                continue
            km = keymap.setdefault((op.phase, op.eng), {})
            k = id(op.key)
            if k not in km:
                km[k] = len(km)
            sk = (op.eng, km[k])
            if sk not in slotd or (slotd[sk][1] >= 60 and slotd[sk][2] != op.phase):
                slotd[sk] = [stack.enter_context(nc.semaphore(f"d{op.eng}{km[k]}_{len(slotd)}_{op.phase}")), 0, op.phase]
            sl = slotd[sk]
            sl[1] += 1
            op.tok = (sl[0], 16 * sl[1])
        for e in ENGS:
            cnt = 0
            sems = []
            for op in self.by_eng[e]:
                if op.fn is None or op.dma:
                    continue
                if op.sig:
                    ep = cnt // EPOCH
                    if ep >= len(sems):
                        sems.append(stack.enter_context(nc.semaphore(f"e_{e}{ep}")))
                    op.tok = (sems[ep], cnt % EPOCH + 1)
                    cnt += 1
        keysem = slots
        self.nsem = len(keysem)
        by_eng = self.by_eng

        def run(e, h):
            waited = {}
            for op in by_eng[e]:
                for d in op.deps:
                    sem, val = d.tok
                    if waited.get(id(sem), 0) >= val:
                        continue
                    h.wait_ge(sem, val)
                    waited[id(sem)] = val
                if op.fn is None:
                    continue
                ins = op.fn(h)
                if op.dma:
                    ins.then_inc(op.tok[0], 16)
                elif op.sig:
                    ins.then_inc(op.tok[0], 1)

        with nc.Block() as block:
            @block.tensor
            def _(h):
                run("pe", h)

            @block.scalar
            def _(h):
                run("act", h)

            @block.vector
            def _(h):
                run("dve", h)

            @block.gpsimd
            def _(h):
                run("pool", h)

            @block.sync
            def _(h):
                run("sp", h)


class Arena:
    def __init__(self, ap, nwords):
        self.ap, self.n, self.off = ap, nwords, 0

    def reset(self):
        self.off = 0

    def alloc(self, shape, dtype=F32):
        nel = int(np.prod(shape))
        nw = (nel * (4 if dtype == F32 else 2) + 3) // 4
        self.off = (self.off + 7) // 8 * 8
        a = self.ap[:, self.off:self.off + nw]
        self.off += nw
        assert self.off <= self.n, f"arena overflow {self.off} > {self.n}"
        if dtype != F32:
            a = a.bitcast(dtype)
            if a.shape[1] != nel:
                a = a[:, 0:nel]
        if len(shape) == 2:
            a = a.rearrange("p (a b) -> p a b", a=shape[0])
        elif len(shape) == 3:
            a = a.rearrange("p (a b c) -> p a b c", a=shape[0], b=shape[1])
        return a


def build(stop=99, nt=8):
    nc = bass.Bass("TRN2", target_bir_lowering=False)
    P = Prog()
    stack = contextlib.ExitStack()

    def din(name, shape, dt=F32):
        return nc.dram_tensor(name, list(shape), dt, kind="ExternalInput").ap()

    def dscr(name, shape, dt=F32):
        return nc.dram_tensor(name, list(shape), dt, kind="Internal").ap()

    x_d = din("x", [SEQ, D])
    ctx_d = din("ctx", [CTX, D])
    cc_d = din("cc", [128, 16])
    ident_d = din("ident", [128, 128])
    wmod_d = din("w_mod", [2, D, 9 * D])
    bmod_d = din("b_mod", [2, 9 * D])
    ng_d = din("norm_g", [2, 6, D])
    ffgu_d = [din("ff1_w_gu", [2, D, 2 * DFF]), din("ff2_w_gu", [2, D, 2 * DFF])]
    ffdn_d = [din("ff1_w_down", [2, DFF, D]), din("ff2_w_down", [2, DFF, D])]
    evin_d = din("ev_w_in", [D, 2560])
    evout_d = din("ev_w_out", [D, D])
    odin_d = din("od_w_in", [D, 3072])
    odout_d = din("od_w_out", [D, D])
    tbl_d = din("tbl", [128, 8 * 14 * 64])
    dww_d = din("dww", [128, 124])
    evs_d = din("evs", [128, 12])
    odw_d = din("odw", [128, 24])
    out_d = nc.dram_tensor("out", [SEQ, D], F32, kind="ExternalOutput").ap()

    xs_d = dscr("xs", [SEQ + CTX, D])
    modd = dscr("modd", [2, 2, 9, D])
    gluT_d = dscr("gluT", [512, SEQ + 32], BF16)
    qT_d = dscr("qT", [512, SEQ], BF16)
    kT_d = dscr("kT", [512, SEQ + CTX], BF16)
    v_d = dscr("vv", [SEQ + CTX, 512], BF16)
    zT_d = dscr("zT", [D, SEQ + 2])
    bT_d = dscr("bT", [D, SEQ])

    NW = 52000
    arena_t = stack.enter_context(nc.sbuf_tensor("arena", [128, NW], F32))
    AR = Arena(arena_t[:, :], NW)
    cst_t = stack.enter_context(nc.sbuf_tensor("cst", [128, 1024], F32))
    cst = cst_t[:, :]
    ident_f = cst[:, 0:128]
    ident_b = cst[:, 128:192].bitcast(BF16)
    ones_b = cst[:, 192:256].bitcast(BF16)
    stats = cst[:, 256:512]
    NST = 256
    st_bufs = [Buf(f"st{i}") for i in range(NST)]
    st_ctr = [0]

    def stat(n=1):
        i = st_ctr[0]
        if i % NST + n > NST:
            i = (i // NST + 1) * NST
        st_ctr[0] = i + n
        i %= NST
        return stats[:, i:i + n], st_bufs[i]

    ps_t = [stack.enter_context(nc.psum_tensor(f"ps{i}", [128, 512], F32)) for i in range(8)]
    ps = [t[:, :] for t in ps_t]
    bps = [Buf(f"ps{i}") for i in range(8)]
    psT = [ps[b].bitcast(BF16)[:, 0:512] for b in (0, 3, 1, 2)]
    bpsT = [bps[0], bps[3], bps[1], bps[2]]

    b_ident = Buf("ident")
    P.add("sp", lambda h: h.dma_start(out=ident_f, in_=ident_d[:, :]), writes=[b_ident], key=b_ident)
    b_identb = Buf("identb")
    P.add("dve", lambda h: h.tensor_copy(out=ident_b, in_=ident_f), reads=[b_ident], writes=[b_identb])
    P.add("dve", lambda h: h.memset(ones_b, 1.0), writes=[b_identb])
    eps_c = cst[:, 512:513]
    P.add("dve", lambda h: h.memset(eps_c, EPS), writes=[b_identb])

    ctr = {"evac": 0}

    def rstd_of(parts, junk):
        ss, bss = stat(2)
        P.add("dve", lambda h: h.memset(ss, 0.0), writes=[bss])
        for i, (ap, b) in enumerate(parts):
            P.add("act", lambda h, i=i, ap=ap: h.activation(out=junk, in_=ap, func=AF.Square, accum_out=ss[:, i:i + 1]),
                  reads=[b, bss], writes=[bss])
        P.add("dve", lambda h: h.tensor_tensor(out=ss[:, 0:1], in0=ss[:, 0:1], in1=ss[:, 1:2], op=ALU.add), reads=[bss], writes=[bss])
        P.add("act", lambda h: h.activation(out=ss[:, 0:1], in_=ss[:, 0:1], func=AF.Sqrt, scale=1.0 / D, bias=eps_c), reads=[bss, b_identb], writes=[bss])
        P.add("dve", lambda h: h.reciprocal(out=ss[:, 0:1], in_=ss[:, 0:1]), reads=[bss], writes=[bss])
        return ss[:, 0:1], bss

    def prenorm(xget, nsub, A_sb, S_sb, bmod, h_sb, bh, junk, tmp, btmp):
        for s in range(nsub):
            xa, bx = xget(s)
            ss, bss = rstd_of([(xa[:, 0:512], bx), (xa[:, 512:1024], bx)], junk)
            t, bt = tmp[s % 2], btmp[s % 2]
            P.add("dve", lambda h, xa=xa, ss=ss, t=t: h.scalar_tensor_tensor(out=t, in0=xa, scalar=ss, in1=A_sb, op0=ALU.mult, op1=ALU.mult),
                  reads=[bx, bss, bmod[0]], writes=[bt])
            P.add("pool", lambda h, s=s, t=t: h.tensor_tensor(out=h_sb[:, s, :], in0=t, in1=S_sb, op=ALU.add),
                  reads=[bt, bmod[1]], writes=[bh[s]])

    def transposes(h_sb, bh, nsub, hT, bhT):
        for c in range(8):
            k = c % 4
            for s in range(nsub):
                P.add("pe", lambda h, c=c, s=s, k=k: h.transpose(out=psT[k][:, s * 128:(s + 1) * 128], in_=h_sb[:, s, c * 128:(c + 1) * 128], identity=ident_b),
                      reads=[bh[s], b_identb], writes=[bpsT[k]])
            n = nsub * 128
            if c % 2 == 0:
                P.add("act", lambda h, c=c, k=k, n=n: h.copy(out=hT[:, c, 0:n], in_=psT[k][:, 0:n]), reads=[bpsT[k]], writes=[bhT[c]])
            else:
                P.add("dve", lambda h, c=c, k=k, n=n: h.tensor_copy(out=hT[:, c, 0:n], in_=psT[k][:, 0:n]), reads=[bpsT[k]], writes=[bhT[c]])

    def postnorm(banks, G_sb, bG, xs_ap, bxs_, junk, tmp, bt):
        ss, bss = rstd_of([(ps[bk], bps[bk]) for bk in banks], junk)
        for i, bk in enumerate(banks):
            P.add("dve", lambda h, i=i, bk=bk: h.scalar_tensor_tensor(out=tmp[:, i * 512:(i + 1) * 512], in0=ps[bk], scalar=ss,
                                                                   in1=G_sb[:, i * 512:(i + 1) * 512], op0=ALU.mult, op1=ALU.mult),
                  reads=[bps[bk], bss, bG], writes=[bt])
        P.add("pool", lambda h: h.tensor_tensor(out=xs_ap, in0=xs_ap, in1=tmp, op=ALU.add), reads=[bt, bxs_], writes=[bxs_])

    def load_mod(l, v, idx, dst, bdst):
        P.add("sp", lambda h: h.dma_start(out=dst, in_=modd[l, v, idx, :].partition_broadcast(128)), reads=[b_modd], writes=[bdst], key=bdst)

    def tile_view(ap, t0, nsub):
        return ap[t0:t0 + nsub * 128, :].rearrange("(s p) d -> p s d", p=128)

    def load_w_cast(dst, src, bdst):
        P.add("pool", lambda h: h.dma_start(out=dst, in_=src), writes=[bdst], key=bdst)

    b_modd = Buf("modd")
    AR.reset()
    ccs = AR.alloc([16])
    sc = AR.alloc([16])
    wm = [AR.alloc([8, 512]) for _ in range(4)]
    bb = [AR.alloc([512]) for _ in range(4)]
    gl = AR.alloc([6 * D])
    mrow = AR.alloc([9 * D])
    rws = AR.alloc([9 * D])
    b_cc, b_sc, b_gl, b_rows = Buf(), Buf(), Buf(), Buf()
    b_wm, b_bb = [Buf() for _ in range(4)], [Buf() for _ in range(4)]
    b_mrow = [Buf() for _ in range(18)]
    P.add("sp", lambda h: h.dma_start(out=ccs, in_=cc_d[:, :]), writes=[b_cc], key=b_cc)
    P.add("act", lambda h: h.activation(out=sc, in_=ccs, func=AF.Silu), reads=[b_cc], writes=[b_sc])
    scv = sc.rearrange("p (k v) -> p k v", v=2)
    for l in range(2):
        P.add("sp", lambda h, l=l: h.dma_start(out=gl[0:2, :], in_=ng_d[l].rearrange("a d -> (a d)").partition_broadcast(2)),
              writes=[b_gl], key=b_gl)
        wv = wmod_d[l].rearrange("(k p) n -> p k n", p=128)
        for blk in range(18):
            k2 = blk % 4
            P.add("sp", lambda h, wv=wv, blk=blk, k2=k2: h.dma_start(out=wm[k2], in_=wv[:, :, blk * 512:(blk + 1) * 512]),
                  writes=[b_wm[k2]], key=b_wm[k2])
            P.add("sp", lambda h, l=l, blk=blk, k2=k2: h.dma_start(out=bb[k2][0:2, :], in_=bmod_d[l, blk * 512:(blk + 1) * 512].partition_broadcast(2)),
                  writes=[b_bb[k2]], key=b_bb[k2])
            for k in range(8):
                P.add("pe", lambda h, k=k, k2=k2: h.matmul(ps[3][0:2, :], lhsT=scv[:, k, :], rhs=wm[k2][:, k, :], start=(k == 0), stop=(k == 7)),
                      reads=[b_sc, b_wm[k2]], writes=[bps[3]])
            P.add("dve", lambda h, blk=blk, k2=k2: h.tensor_tensor(out=mrow[0:2, blk * 512:(blk + 1) * 512], in0=ps[3][0:2, :], in1=bb[k2][0:2, :], op=ALU.add),
                  reads=[bps[3], b_bb[k2]], writes=[b_mrow[blk]])
        for s in range(3):
            res = 1.0 if s == 1 else 0.5

            def seg(a, i):
                return a[0:2, i * D:(i + 1) * D]
            rb = [b_mrow[2 * (3 * s + i)] for i in range(3)] + [b_mrow[2 * (3 * s + i) + 1] for i in range(3)]
            P.add("dve", lambda h, s=s, seg=seg: h.scalar_tensor_tensor(out=seg(rws, 3 * s), in0=seg(mrow, 3 * s + 1), scalar=1.0, in1=seg(gl, 2 * s),
                                                                     op0=ALU.add, op1=ALU.mult), reads=rb + [b_gl], writes=[b_rows])
            P.add("dve", lambda h, s=s, seg=seg: h.tensor_copy(out=seg(rws, 3 * s + 1), in_=seg(mrow, 3 * s)), reads=rb, writes=[b_rows])
            P.add("dve", lambda h, s=s, seg=seg, res=res: h.scalar_tensor_tensor(out=seg(rws, 3 * s + 2), in0=seg(mrow, 3 * s + 2), scalar=res, in1=seg(gl, 2 * s + 1),
                                                                              op0=ALU.mult, op1=ALU.mult), reads=rb + [b_gl], writes=[b_rows])
        P.add("sp", lambda h, l=l: h.dma_start(out=modd[l].rearrange("v a d -> v (a d)"), in_=rws[0:2, :]), reads=[b_rows], writes=[b_modd], key=b_rows)
    P.barrier()

    def ffn(l, s_idx, wgu_src, wdn_src, tiles):
        AR.reset()
        Wgu = AR.alloc([8, 2 * DFF], BF16)
        Wd = AR.alloc([NFC, D], BF16)
        xr = [AR.alloc([D]) for _ in range(3)]
        hs = AR.alloc([4, D], BF16)
        hT = AR.alloc([8, 512], BF16)
        actT = AR.alloc([NFC, 512], BF16)
        sg0 = AR.alloc([512])
        sg = [sg0, sg0]
        tmp0 = AR.alloc([D])
        tmp = [tmp0, tmp0]
        junk = AR.alloc([512], BF16)
        A_sb, S_sb, G_sb = AR.alloc([D]), AR.alloc([D]), AR.alloc([D])
        bA, bS, bG = Buf("A"), Buf("S"), Buf("G")
        bxr = [Buf() for _ in range(3)]
        xr_ctr = [0]
        bh = [Buf() for _ in range(4)]
        bhT = [Buf() for _ in range(8)]
        bact = [Buf() for _ in range(NFC)]
        bsg0, btmp0 = Buf(), Buf()
        bsg = [bsg0, bsg0]
        btmp = [btmp0, btmp0]
        bWg = [Buf() for _ in range(11)]
        bWu = [Buf() for _ in range(11)]
        bWd = [Buf() for _ in range(11)]
        gv = wgu_src.rearrange("(k p) n -> p k n", p=128)
        dv = wdn_src.rearrange("(j p) d -> p j d", p=128)
        for jj in range(11):
            load_w_cast(Wgu[:, :, jj * 256:(jj + 1) * 256], gv[:, :, jj * 256:(jj + 1) * 256], bWg[jj])
            load_w_cast(Wgu[:, :, DFF + jj * 256:DFF + (jj + 1) * 256], gv[:, :, DFF + jj * 256:DFF + (jj + 1) * 256], bWu[jj])
        for jj in range(11):
            load_w_cast(Wd[:, 2 * jj:2 * jj + 2, :], dv[:, 2 * jj:2 * jj + 2, :], bWd[jj])
        cur_v = [None, None]

        def ensure_mod_pre(v):
            if cur_v[0] != v:
                load_mod(l, v, 3 * s_idx + 0, A_sb, bA)
                load_mod(l, v, 3 * s_idx + 1, S_sb, bS)
                cur_v[0] = v

        def ensure_mod_post(v):
            if cur_v[1] != v:
                load_mod(l, v, 3 * s_idx + 2, G_sb, bG)
                cur_v[1] = v

        def xload(i, s):
            src, dst, nsub, v, sb, db = tiles[i]
            k = xr_ctr[0] % 3
            xr_ctr[0] += 1
            P.add("sp", lambda h: h.dma_start(out=xr[k], in_=src[s * 128:(s + 1) * 128, :]), reads=[sb[s]], writes=[bxr[k]], key=bxr[k])
            return xr[k], bxr[k]

        def pre(i):
            src, dst, nsub, v, sb, db = tiles[i]
            ensure_mod_pre(v)
            prenorm(lambda s: xload(i, s), nsub, A_sb, S_sb, (bA, bS), hs, bh, junk, tmp, btmp)

        n = len(tiles)
        pre(0)
        for i in range(n):
            src, dst, nsub, v, sb, db = tiles[i]
            N = nsub * 128
            if i == 0:
                transposes(hs, bh, nsub, hT, bhT)
            for j in range(NFC):
                for (off, bk, bw) in ((0, 1, bWg[j // 2]), (DFF, 2, bWu[j // 2])):
                    for c in range(8):
                        P.add("pe", lambda h, c=c, j=j, off=off, bk=bk, N=N: h.matmul(ps[bk][:, 0:N], lhsT=Wgu[:, c, off + j * 128:off + (j + 1) * 128],
                                                                                      rhs=hT[:, c, 0:N], start=(c == 0), stop=(c == 7)),
                              reads=[bw, bhT[c]], writes=[bps[bk]])
                k = j % 2
                P.add("act", lambda h, k=k, N=N: h.activation(out=sg[k][:, 0:N], in_=ps[1][:, 0:N], func=AF.Silu), reads=[bps[1]], writes=[bsg[k]])
                P.add("dve", lambda h, k=k, j=j, N=N: h.tensor_tensor(out=actT[:, j, 0:N], in0=ps[2][:, 0:N], in1=sg[k][:, 0:N], op=ALU.mult),
                      reads=[bps[2], bsg[k]], writes=[bact[j]])
                if j == 8 and i + 1 < n:
                    pre(i + 1)
            ensure_mod_post(v)
            for s in range(nsub):
                b0 = 4 + (s % 2) * 2
                for half in range(2):
                    bk = b0 + half
                    for j in range(NFC):
                        P.add("pe", lambda h, j=j, s=s, bk=bk, half=half: h.matmul(ps[bk], lhsT=actT[:, j, s * 128:(s + 1) * 128],
                                                                                   rhs=Wd[:, j, half * 512:(half + 1) * 512], start=(j == 0), stop=(j == NFC - 1)),
                              reads=[bact[j], bWd[j // 2]], writes=[bps[bk]])
                if s == nsub - 1 and i + 1 < n:
                    transposes(hs, bh, tiles[i + 1][2], hT, bhT)
                xa, bxa = xload(i, s)
                postnorm((b0, b0 + 1), G_sb, bG, xa, bxa, junk, tmp[s % 2], btmp[s % 2])
                P.add("sp", lambda h, xa=xa, dst=dst, s=s: h.dma_start(out=dst[s * 128:(s + 1) * 128, :], in_=xa), reads=[bxa], writes=[db[s]], key=bxa)
        P.barrier()

    bxs = [Buf(f"xs{i}") for i in range(34)]

    def rows(ap, t0, n):
        return ap[t0:t0 + n, :]
    lat_tiles_in = [(rows(x_d, i * 512, 512), rows(xs_d, i * 512, 512), 4, 0, [Buf()] * 4, bxs[4 * i:4 * i + 4]) for i in range(8)]
    ctx_tile_in = (rows(ctx_d, 0, 256), rows(xs_d, SEQ, 256), 2, 1, [Buf()] * 2, bxs[32:34])
    lat_tiles = [(rows(xs_d, i * 512, 512), rows(xs_d, i * 512, 512), 4, 0, bxs[4 * i:4 * i + 4], bxs[4 * i:4 * i + 4]) for i in range(8)]
    lat_tiles_out = [(rows(xs_d, i * 512, 512), rows(out_d, i * 512, 512), 4, 0, bxs[4 * i:4 * i + 4], [Buf() for _ in range(4)]) for i in range(8)]

    def mixer_front(l, tiles, AR_bufs):
        pass

    def out_proj_post(yT, byT, Wout, bWout, xt_ap, bxt_, G_sb, bG, junk, tmp, btmp, nsub=4, b_base=2):
        for s in range(nsub):
            b0 = b_base + (s % 2) * 2
            for half in range(2):
                for c in range(8):
                    P.add("pe", lambda h, s=s, c=c, half=half, b0=b0: h.matmul(ps[b0 + half], lhsT=yT[:, c, s * 128:(s + 1) * 128],
                                                                               rhs=Wout[:, c, half * 512:(half + 1) * 512], start=(c == 0), stop=(c == 7)),
                          reads=[byT[c], bWout], writes=[bps[b0 + half]])
            postnorm((b0, b0 + 1), G_sb, bG, xt_ap[:, s, :], bxt_, junk, tmp[s % 2], btmp[s % 2])

    def even_mixer():
        l = 0
        AR.reset()
        Wout = AR.alloc([8, D], BF16)
        A_sb, S_sb, G_sb = AR.alloc([D]), AR.alloc([D]), AR.alloc([D])
        xt0 = AR.alloc([4, D])
        xt = [xt0, xt0]
        hs = AR.alloc([4, D], BF16)
        hT = AR.alloc([8, 512], BF16)
        tmp = [AR.alloc([D]) for _ in range(2)]
        junk = AR.alloc([512], BF16)
        base_off = AR.off
        Win = AR.alloc([8, 2560], BF16)
        bWin = [Buf() for _ in range(5)]
        bWout, btbl, bdiag, bB, bdww, bevs = Buf(), Buf(), Buf(), Buf(), Buf(), Buf()
        bA, bS, bG = Buf(), Buf(), Buf()
        bxt0 = Buf()
        bxt = [bxt0, bxt0]
        bh = [Buf() for _ in range(4)]
        bhT = [Buf() for _ in range(8)]
        btmp = [Buf(), Buf()]
        wv = evin_d.rearrange("(k p) n -> p k n", p=128)
        for g in range(5):
            load_w_cast(Win[:, :, g * 512:(g + 1) * 512], wv[:, :, g * 512:(g + 1) * 512], bWin[g])
        load_w_cast(Wout, evout_d.rearrange("(k p) n -> p k n", p=128), bWout)
        load_mod(0, 0, 3, A_sb, bA)
        load_mod(0, 0, 4, S_sb, bS)
        load_mod(0, 0, 5, G_sb, bG)

        glu_sb = AR.alloc([4, 512], BF16)
        q_sb = AR.alloc([4, 512], BF16)
        k_sb = AR.alloc([4, 512], BF16)
        v_sb = AR.alloc([4, 512], BF16)
        sgm = [AR.alloc([512]) for _ in range(2)]
        zpad = AR.alloc([16], BF16)
        bglu, bq, bk_, bv = Buf(), Buf(), Buf(), Buf()
        bsgm = [Buf(), Buf()]
        bzp = Buf()
        d_glu = [Buf() for _ in range(8)]
        d_q = [Buf() for _ in range(8)]
        d_k = [Buf() for _ in range(9)]
        d_v = [Buf() for _ in range(9)]
        d_pad = Buf()
        gluv = gluT_d.rearrange("(c p) t -> p c t", p=128)
        qv = qT_d.rearrange("(c p) t -> p c t", p=128)
        kv = kT_d.rearrange("(c p) t -> p c t", p=128)
        P.add("dve", lambda h: h.memset(zpad, 0.0), writes=[bzp])
        for c in range(4):
            P.add("sp", lambda h, c=c: h.dma_start(out=gluT_d[c * 128:(c + 1) * 128, 0:16], in_=zpad), reads=[bzp], writes=[d_pad], key=bzp)
            P.add("sp", lambda h, c=c: h.dma_start(out=gluT_d[c * 128:(c + 1) * 128, 15 + SEQ:15 + SEQ + 16], in_=zpad), reads=[bzp], writes=[d_pad], key=bzp)

        def proj(fc, bk, N):
            g = fc // 4
            for c in range(8):
                P.add("pe", lambda h, c=c: h.matmul(ps[bk][:, 0:N], lhsT=Win[:, c, fc * 128:(fc + 1) * 128], rhs=hT[:, c, 0:N], start=(c == 0), stop=(c == 7)),
                      reads=[bWin[g], bhT[c]], writes=[bps[bk]])

        hs2 = AR.alloc([4, D], BF16)
        xr = [AR.alloc([D]) for _ in range(3)]
        bxr = [Buf() for _ in range(3)]
        xr_ctr = [0]
        hsb = [hs, hs2]
        bhb = [bh, [Buf() for _ in range(4)]]
        tiles_e1 = list(range(min(nt + 1, 8))) + [8]

        def xload_e1(j, s):
            k = xr_ctr[0] % 3
            xr_ctr[0] += 1
            P.add("sp", lambda h: h.dma_start(out=xr[k], in_=xs_d[j * 512 + s * 128:j * 512 + (s + 1) * 128, :]), reads=[bxs[4 * j + s]], writes=[bxr[k]], key=bxr[k])
            return xr[k], bxr[k]

        def pre_e1(idx):
            j = tiles_e1[idx]
            nsub = 4 if j < 8 else 2
            if j == 8:
                load_mod(0, 1, 3, A_sb, bA)
                load_mod(0, 1, 4, S_sb, bS)
            prenorm(lambda s: xload_e1(j, s), nsub, A_sb, S_sb, (bA, bS), hsb[idx % 2], bhb[idx % 2], junk, tmp, btmp)

        pre_e1(0)
        for idx, j in enumerate(tiles_e1):
            nsub = 4 if j < 8 else 2
            N = nsub * 128
            t0 = j * 512
            transposes(hsb[idx % 2], bhb[idx % 2], nsub, hT, bhT)
            if idx + 1 < len(tiles_e1):
                pre_e1(idx + 1)
            if j < 8:
                for ch in range(4):
                    k = ch % 2
                    proj(4 + ch, 1, N)
                    P.add("act", lambda h, k=k: h.activation(out=sgm[k], in_=ps[1], func=AF.Sigmoid), reads=[bps[1]], writes=[bsgm[k]])
                    proj(ch, 2, N)
                    P.add("dve", lambda h, k=k, ch=ch: h.tensor_tensor(out=glu_sb[:, ch, :], in0=ps[2], in1=sgm[k], op=ALU.mult),
                          reads=[bps[2], bsgm[k]], writes=[bglu])
                P.add("sp", lambda h, t0=t0: h.dma_start(out=gluv[:, :, 15 + t0:15 + t0 + 512], in_=glu_sb), reads=[bglu, d_pad], writes=[d_glu[j]], key=bglu)
                for ch in range(4):
                    bk = 1 + ch % 2
                    proj(8 + ch, bk, N)
                    P.add("act", lambda h, bk=bk, ch=ch: h.activation(out=q_sb[:, ch, :], in_=ps[bk], func=AF.Copy, scale=0.125), reads=[bps[bk]], writes=[bq])
                P.add("sp", lambda h, t0=t0: h.dma_start(out=qv[:, :, t0:t0 + 512], in_=q_sb), reads=[bq], writes=[d_q[j]], key=bq)
            for ch in range(4):
                bk = 1 + ch % 2
                proj(12 + ch, bk, N)
                P.add("dve", lambda h, bk=bk, ch=ch, N=N: h.tensor_copy(out=k_sb[:, ch, 0:N], in_=ps[bk][:, 0:N]), reads=[bps[bk]], writes=[bk_])
            P.add("sp", lambda h, t0=t0, N=N: h.dma_start(out=kv[:, :, t0:t0 + N], in_=k_sb[:, :, 0:N]), reads=[bk_], writes=[d_k[j]], key=bk_)
            for s in range(nsub):
                bk = 4 + s
                for c in range(8):
                    P.add("pe", lambda h, c=c, s=s, bk=bk: h.matmul(ps[bk], lhsT=hT[:, c, s * 128:(s + 1) * 128], rhs=Win[:, c, 2048:2560], start=(c == 0), stop=(c == 7)),
                          reads=[bWin[4], bhT[c]], writes=[bps[bk]])
                if s % 2 == 0:
                    P.add("act", lambda h, s=s, bk=bk: h.copy(out=v_sb[:, s, :], in_=ps[bk]), reads=[bps[bk]], writes=[bv])
                else:
                    P.add("dve", lambda h, s=s, bk=bk: h.tensor_copy(out=v_sb[:, s, :], in_=ps[bk]), reads=[bps[bk]], writes=[bv])
            P.add("sp", lambda h, t0=t0, nsub=nsub: h.dma_start(out=v_d[t0:t0 + nsub * 128, :].rearrange("(s p) d -> p s d", p=128), in_=v_sb[:, 0:nsub, :]),
                  reads=[bv], writes=[d_v[j]], key=bv)
        P.barrier()
        AR.off = base_off
        tbl = AR.alloc([8, 14, 64])
        diag = AR.alloc([124, 128], BF16)
        Bmat = AR.alloc([128])
        dww = AR.alloc([124])
        evs = AR.alloc([12])
        P.add("sp", lambda h: h.dma_start(out=tbl, in_=tbl_d.rearrange("p (a b c) -> p a b c", a=8, b=14)), writes=[btbl], key=btbl)
        P.add("sp", lambda h: h.dma_start(out=dww, in_=dww_d[:, :]), writes=[bdww], key=bdww)
        P.add("sp", lambda h: h.dma_start(out=evs, in_=evs_d[:, :]), writes=[bevs], key=bevs)
        for idx in range(124):
            P.add("dve", lambda h, idx=idx: h.tensor_scalar(out=diag[:, idx, :], in0=ident_f, scalar1=dww[:, idx:idx + 1], scalar2=None, op0=ALU.mult),
                  reads=[bdww, b_ident], writes=[bdiag])
        P.add("dve", lambda h: h.memset(Bmat, 0.0), writes=[bB])
        P.add("dve", lambda h: h.memset(Bmat[0:64, 0:64], 1.0 / 64), writes=[bB])
        P.add("dve", lambda h: h.memset(Bmat[64:128, 64:128], 1.0 / 64), writes=[bB])
        gw = [AR.alloc([544], BF16) for _ in range(2)]
        cv = AR.alloc([512])
        dd = AR.alloc([512])
        sq = AR.alloc([512])
        rs = AR.alloc([512])
        yab = AR.alloc([8, 512], BF16)
        QTt = AR.alloc([4, 512], BF16)
        KTw = AR.alloc([4, 1024], BF16)
        Vw = AR.alloc([8, 512], BF16)
        KTc = AR.alloc([4, 256], BF16)
        Vc = AR.alloc([2, 512], BF16)
        sbA = [AR.alloc([512]) for _ in range(2)]
        sbB = [AR.alloc([128]) for _ in range(2)]
        PT = [AR.alloc([7, 8, 64], BF16) for _ in range(2)]
        Qbd = AR.alloc([4, 8, 128], BF16)
        bQbd = Buf()
        P.add("pool", lambda h: h.memset(Qbd, 0.0), writes=[bQbd])
        rden = AR.alloc([512])
        bgw = [Buf(), Buf()]
        bcv, bdd, bsq, brs = Buf(), Buf(), Buf(), Buf()
        byab = [Buf() for _ in range(8)]
        bQ, bK, bV, bKc, bVc = Buf(), Buf(), Buf(), Buf(), Buf()
        bsbA, bsbB = [Buf(), Buf()], [Buf(), Buf()]
        bPT = [[Buf() for _ in range(4)] for _ in range(2)]
        brden = Buf()
        P.add("sp", lambda h: h.dma_start(out=KTc, in_=kv[:, :, SEQ:SEQ + CTX]), reads=[d_k[8]], writes=[bKc], key=bKc)
        P.add("sp", lambda h: h.dma_start(out=Vc, in_=v_d[SEQ:SEQ + CTX, :].rearrange("(s p) d -> p s d", p=128)), reads=[d_v[8]], writes=[bVc], key=bVc)
        tblv = tbl
        for j in range(nt):
            t0 = j * 512
            win_lo = min(max(8 * j - 4, 0), 48)
            def conv(ch):
                k = ch % 2
                P.add("sp", lambda h, ch=ch, k=k, t0=t0: h.dma_start(out=gw[k][:, 0:542], in_=gluT_d[ch * 128:(ch + 1) * 128, t0:t0 + 542]),
                      reads=[d_glu[max(j - 1, 0)], d_glu[j], d_glu[min(j + 1, 7)], d_pad], writes=[bgw[k]], key=bgw[k])
                for tap in range(31):
                    P.add("pe", lambda h, ch=ch, k=k, tap=tap: h.matmul(ps[k], lhsT=diag[:, ch * 31 + tap, :], rhs=gw[k][:, tap:tap + 512], start=(tap == 0), stop=(tap == 30)),
                          reads=[bdiag, bgw[k]], writes=[bps[k]])

            conv(0)
            for ch in range(4):
                k = ch % 2
                if ch + 1 < 4:
                    conv(ch + 1)
                P.add("act", lambda h, ch=ch, k=k: h.activation(out=cv, in_=ps[k], func=AF.Identity, bias=evs[:, ch:ch + 1]), reads=[bps[k], bevs], writes=[bcv])
                P.add("pe", lambda h: h.matmul(ps[6], lhsT=Bmat, rhs=cv, start=True, stop=True), reads=[bB, bcv], writes=[bps[6]])
                P.add("dve", lambda h: h.tensor_tensor(out=dd, in0=cv, in1=ps[6], op=ALU.subtract), reads=[bcv, bps[6]], writes=[bdd])
                P.add("act", lambda h: h.activation(out=sq, in_=dd, func=AF.Square), reads=[bdd], writes=[bsq])
                P.add("pe", lambda h: h.matmul(ps[7], lhsT=Bmat, rhs=sq, start=True, stop=True), reads=[bB, bsq], writes=[bps[7]])
                P.add("act", lambda h: h.activation(out=rs, in_=ps[7], func=AF.Sqrt, bias=eps_c), reads=[bps[7]], writes=[brs])
                P.add("dve", lambda h: h.reciprocal(out=rs, in_=rs), reads=[brs], writes=[brs])
                P.add("dve", lambda h: h.tensor_tensor(out=dd, in0=dd, in1=rs, op=ALU.mult), reads=[bdd, brs], writes=[bdd])
                P.add("act", lambda h, ch=ch: h.activation(out=yab[:, ch, :], in_=dd, func=AF.Silu, scale=evs[:, 4 + ch:5 + ch], bias=evs[:, 8 + ch:9 + ch]),
                      reads=[bdd, bevs], writes=[byab[ch]])
            P.add("sp", lambda h, t0=t0: h.dma_start(out=QTt, in_=qv[:, :, t0:t0 + 512]), reads=[d_q[j]], writes=[bQ], key=bQ)
            for par in range(2):
                p0 = par * 64
                P.add("pool", lambda h, p0=p0: h.tensor_copy(out=Qbd[p0:p0 + 64, :, :, p0:p0 + 64], in_=QTt[p0:p0 + 64, :, :].rearrange("p c (r q) -> p c r q", q=64)),
                      reads=[bQ], writes=[bQbd])
            wt0 = win_lo * 64
            wtiles = sorted(set([wt0 // 512, (wt0 + 1023) // 512]))
            P.add("sp", lambda h, wt0=wt0: h.dma_start(out=KTw, in_=kv[:, :, wt0:wt0 + 1024]), reads=[d_k[t] for t in wtiles], writes=[bK], key=bK)
            P.add("sp", lambda h, wt0=wt0: h.dma_start(out=Vw, in_=v_d[wt0:wt0 + 1024, :].rearrange("(s p) d -> p s d", p=128)),
                  reads=[d_v[t] for t in wtiles], writes=[bV], key=bV)
            def scores(rr):
                r = 8 * j + rr
                rs_ = min(max(r - 4, 0), 56)
                if rs_ % 2 == 0:
                    starts = [rs_ + 2 * t for t in range(4)]
                    kr = [(0, 128)] * 4
                else:
                    starts = [rs_ - 1 + 2 * t for t in range(5)]
                    kr = [(64, 128), (0, 128), (0, 128), (0, 128), (0, 64)]
                ntl = len(starts)
                e0 = starts[0] - r + 7
                assert 0 <= e0 and e0 + 2 * (ntl - 1) <= 13
                pb = r % 2
                n4 = min(ntl, 4)
                for g in range(4):
                    kk3 = (rr * 4 + g) % 3
                    bA_, bB_ = 2 * kk3, 2 * kk3 + 1
                    k2 = g % 2
                    for t in range(ntl):
                        ko = (starts[t] - win_lo) * 64
                        dstb, dlo = (bA_, t * 128) if t < 4 else (bB_, 0)
                        P.add("pe", lambda h, ko=ko, g=g, dstb=dstb, dlo=dlo, rr=rr: h.matmul(ps[dstb][:, dlo:dlo + 128], lhsT=KTw[:, g, ko:ko + 128],
                                                                                              rhs=Qbd[:, g, rr, :], start=True, stop=True),
                              reads=[bK, bQbd], writes=[bps[dstb]])
                    for cx in range(2):
                        P.add("pe", lambda h, cx=cx, g=g, bB_=bB_, rr=rr: h.matmul(ps[bB_][:, (1 + cx) * 128:(2 + cx) * 128], lhsT=KTc[:, g, cx * 128:(cx + 1) * 128],
                                                                                   rhs=Qbd[:, g, rr, :], start=True, stop=True),
                              reads=[bKc, bQbd], writes=[bps[bB_]])
                    P.add("dve", lambda h, g=g, k2=k2, n4=n4, e0=e0, bA_=bA_: h.tensor_tensor(
                        out=sbA[k2][:, 0:n4 * 128].rearrange("p (a b c) -> p a b c", b=2, c=64),
                        in0=ps[bA_][:, 0:n4 * 128].rearrange("p (a b c) -> p a b c", b=2, c=64),
                        in1=tblv[:, 2 * g:2 * g + 2, e0:e0 + 2 * n4 - 1:2, :].transpose([0, 2, 1, 3]), op=ALU.add),
                        reads=[bps[bA_], btbl], writes=[bsbA[k2]])
                    P.add("act", lambda h, g=g, k2=k2, n4=n4, pb=pb: h.activation(out=PT[pb][:, 0:n4, 2 * g:2 * g + 2, :],
                                                                                 in_=sbA[k2][:, 0:n4 * 128].rearrange("p (a b c) -> p a b c", b=2, c=64), func=AF.Exp),
                          reads=[bsbA[k2]], writes=[bPT[pb][g]])
                    if ntl == 5:
                        P.add("dve", lambda h, g=g, k2=k2, e0=e0, bB_=bB_: h.tensor_tensor(
                            out=sbB[k2].rearrange("p (b c) -> p b c", c=64), in0=ps[bB_][:, 0:128].rearrange("p (b c) -> p b c", c=64),
                            in1=tblv[:, 2 * g:2 * g + 2, e0 + 8, :], op=ALU.add), reads=[bps[bB_], btbl], writes=[bsbB[k2]])
                        P.add("act", lambda h, g=g, k2=k2, pb=pb: h.activation(out=PT[pb][:, 4, 2 * g:2 * g + 2, :], in_=sbB[k2].rearrange("p (b c) -> p b c", c=64), func=AF.Exp),
                              reads=[bsbB[k2]], writes=[bPT[pb][g]])
                    P.add("act", lambda h, g=g, pb=pb, bB_=bB_: h.activation(out=PT[pb][:, 5:7, 2 * g:2 * g + 2, :],
                                                                             in_=ps[bB_][:, 128:384].rearrange("p (a b c) -> p a b c", b=2, c=64), func=AF.Exp),
                          reads=[bps[bB_]], writes=[bPT[pb][g]])
            def pvpart(rr):
                r = 8 * j + rr
                rs_ = min(max(r - 4, 0), 56)
                if rs_ % 2 == 0:
                    starts = [rs_ + 2 * t for t in range(4)]
                    kr = [(0, 128)] * 4
                else:
                    starts = [rs_ - 1 + 2 * t for t in range(5)]
                    kr = [(64, 128), (0, 128), (0, 128), (0, 128), (0, 64)]
                ntl = len(starts)
                e0 = starts[0] - r + 7
                assert 0 <= e0 and e0 + 2 * (ntl - 1) <= 13
                pb = r % 2
                items = [(kr[t], ("w", (starts[t] - win_lo) // 2), t) for t in range(ntl)] + [((0, 128), ("c", 0), 5), ((0, 128), ("c", 1), 6)]
                for g in range(4):
                    for ii, ((p0, p1), (kind, vi), t) in enumerate(items):
                        vsrc, bvs = (Vw, bV) if kind == "w" else (Vc, bVc)
                        P.add("pe", lambda h, g=g, p0=p0, p1=p1, vsrc=vsrc, vi=vi, t=t, ii=ii, pb=pb, nit=len(items): h.matmul(
                            ps[6][:, g * 128:(g + 1) * 128], lhsT=vsrc[p0:p1, vi, g * 128:(g + 1) * 128], rhs=PT[pb][p0:p1, t, 2 * g:2 * g + 2, :],
                            start=(ii == 0), stop=(ii == nit - 1)), reads=[bvs, bPT[pb][g]], writes=[bps[6]])
                for ii, ((p0, p1), _, t) in enumerate(items):
                    P.add("pe", lambda h, p0=p0, p1=p1, t=t, ii=ii, pb=pb, nit=len(items): h.matmul(ps[7], lhsT=ones_b[p0:p1, :], rhs=PT[pb][p0:p1, t, :, :],
                                                                                    start=(ii == 0), stop=(ii == nit - 1)),
                          reads=[b_identb] + bPT[pb], writes=[bps[7]])
                P.add("dve", lambda h: h.reciprocal(out=rden, in_=ps[7]), reads=[bps[7]], writes=[brden])
                for par in range(2):
                    p0 = par * 64
                    P.add("dve", lambda h, par=par, p0=p0, rr=rr: h.tensor_tensor(
                        out=yab[p0:p0 + 64, 4:8, rr * 64:(rr + 1) * 64],
                        in0=ps[6][p0:p0 + 64, :].rearrange("p (g two q) -> p g two q", two=2, q=64)[:, :, par, :],
                        in1=rden[p0:p0 + 64, :].rearrange("p (g two q) -> p g two q", two=2, q=64)[:, :, par, :], op=ALU.mult),
                        reads=[bps[6], brden], writes=byab[4:8])
            scores(0)
            for rr in range(8):
                if rr + 1 < 8:
                    scores(rr + 1)
                pvpart(rr)
            P.add("sp", lambda h, j=j, t0=t0: h.dma_start(out=xt[j % 2], in_=tile_view(xs_d, t0, 4)), reads=bxs[4 * j:4 * j + 4], writes=[bxt[j % 2]], key=bxt[j % 2])
            out_proj_post(yab, byab, Wout, bWout, xt[j % 2], bxt[j % 2], G_sb, bG, junk, tmp, btmp)
            P.add("sp", lambda h, j=j, t0=t0: h.dma_start(out=tile_view(xs_d, t0, 4), in_=xt[j % 2]), reads=[bxt[j % 2]], writes=bxs[4 * j:4 * j + 4], key=bxt[j % 2])
        P.barrier()

    def odd_mixer():
        l = 1
        AR.reset()
        Win = AR.alloc([8, 3072], BF16)
        Wout = AR.alloc([8, D], BF16)
        odw = AR.alloc([24])
        A_sb, S_sb, G_sb = AR.alloc([D]), AR.alloc([D]), AR.alloc([D])
        xr = [AR.alloc([D]) for _ in range(3)]
        hs = [AR.alloc([4, D], BF16) for _ in range(2)]
        hT = AR.alloc([8, 512], BF16)
        tmp = [AR.alloc([D]) for _ in range(2)]
        junk = AR.alloc([512], BF16)
        zs1 = [AR.alloc([512]) for _ in range(2)]
        bs1 = [AR.alloc([512]) for _ in range(2)]
        csb = [AR.alloc([512]) for _ in range(2)]
        zs2 = [AR.alloc([520]) for _ in range(2)]
        bs2 = [AR.alloc([512]) for _ in range(2)]
        acc = [AR.alloc([512]) for _ in range(2)]
        gT = AR.alloc([8, 512], BF16)
        xt = AR.alloc([4, D])
        zp = AR.alloc([8])
        bWin = [Buf() for _ in range(6)]
        bWout, bodw = Buf(), Buf()
        bA, bS, bG = Buf(), Buf(), Buf()
        bxr = [Buf() for _ in range(3)]
        xr_ctr = [0]
        bh = [[Buf() for _ in range(4)] for _ in range(2)]
        bhT = [Buf() for _ in range(8)]
        btmp = [Buf(), Buf()]
        bzs1, bbs1, bcsb, bzs2, bbs2, bacc = ([Buf(), Buf()] for _ in range(6))
        bgT = [Buf() for _ in range(8)]
        bxt, bzp = Buf(), Buf()
        d_z = [[Buf() for _ in range(8)] for _ in range(8)]
        d_b = [[Buf() for _ in range(8)] for _ in range(8)]
        d_pad = Buf()
        wv = odin_d.rearrange("(k p) n -> p k n", p=128)
        for g in (2, 4, 0, 3, 5, 1):
            load_w_cast(Win[:, :, g * 512:(g + 1) * 512], wv[:, :, g * 512:(g + 1) * 512], bWin[g])
        load_w_cast(Wout, odout_d.rearrange("(k p) n -> p k n", p=128), bWout)
        P.add("sp", lambda h: h.dma_start(out=odw, in_=odw_d[:, :]), writes=[bodw], key=bodw)
        load_mod(1, 0, 3, A_sb, bA)
        load_mod(1, 0, 4, S_sb, bS)
        load_mod(1, 0, 5, G_sb, bG)
        zv = zT_d.rearrange("(c p) t -> p c t", p=128)
        P.add("dve", lambda h: h.memset(zp, 0.0), writes=[bzp])
        P.add("sp", lambda h: h.dma_start(out=zv[:, :, 0:1], in_=zp.rearrange("p (a b) -> p a b", b=1), allow_slow_non_contiguous=True), reads=[bzp], writes=[d_pad], key=bzp)
        P.add("sp", lambda h: h.dma_start(out=zv[:, :, SEQ + 1:SEQ + 2], in_=zp.rearrange("p (a b) -> p a b", b=1), allow_slow_non_contiguous=True), reads=[bzp], writes=[d_pad], key=bzp)

        def proj(fc, bk):
            g = fc // 4
            for c in range(8):
                P.add("pe", lambda h, c=c: h.matmul(ps[bk], lhsT=Win[:, c, fc * 128:(fc + 1) * 128], rhs=hT[:, c, :], start=(c == 0), stop=(c == 7)),
                      reads=[bWin[g], bhT[c]], writes=[bps[bk]])

        def xload(j, s):
            k = xr_ctr[0] % 3
            xr_ctr[0] += 1
            P.add("sp", lambda h: h.dma_start(out=xr[k], in_=xs_d[j * 512 + s * 128:j * 512 + (s + 1) * 128, :]), reads=[bxs[4 * j + s]], writes=[bxr[k]], key=bxr[k])
            return xr[k], bxr[k]

        def pre(j):
            prenorm(lambda s: xload(j, s), 4, A_sb, S_sb, (bA, bS), hs[j % 2], bh[j % 2], junk, tmp, btmp)

        NT1 = min(nt + 1, 8)
        pre(0)
        for j in range(NT1 + 1):
            do1 = j < NT1
            do2 = 1 <= j <= nt
            if do1:
                transposes(hs[j % 2], bh[j % 2], 4, hT, bhT)
                if j + 1 < NT1:
                    pre(j + 1)
            for chh in range(9):
                ch = chh
                k = ch % 2
                if do1 and chh < 8:
                    t0 = j * 512
                    proj(8 + ch, 1)
                    P.add("act", lambda h, k=k: h.copy(out=csb[k], in_=ps[1]), reads=[bps[1]], writes=[bcsb[k]])
                    proj(16 + ch, 2)
                    P.add("dve", lambda h, k=k: h.tensor_tensor(out=zs1[k], in0=ps[2], in1=csb[k], op=ALU.mult), reads=[bps[2], bcsb[k]], writes=[bzs1[k]])
                    P.add("sp", lambda h, k=k, ch=ch, t0=t0: h.dma_start(out=zT_d[ch * 128:(ch + 1) * 128, 1 + t0:1 + t0 + 512], in_=zs1[k]),
                          reads=[bzs1[k], d_pad], writes=[d_z[j][ch]], key=bzs1[k])
                    proj(ch, 3)
                    P.add("act", lambda h, k=k: h.copy(out=bs1[k], in_=ps[3]), reads=[bps[3]], writes=[bbs1[k]])
                    P.add("sp", lambda h, k=k, ch=ch, t0=t0: h.dma_start(out=bT_d[ch * 128:(ch + 1) * 128, t0:t0 + 512], in_=bs1[k]),
                          reads=[bbs1[k]], writes=[d_b[j][ch]], key=bbs1[k])
                if do2 and chh >= 1:
                    ch = chh - 1
                    k = ch % 2
                    jj = j - 1
                    t1 = jj * 512
                    zdeps = [d_z[max(jj - 1, 0)][ch], d_z[jj][ch], d_z[min(jj + 1, 7)][ch], d_pad]
                    P.add("sp", lambda h, k=k, ch=ch, t1=t1: h.dma_start(out=zs2[k][:, 0:514], in_=zT_d[ch * 128:(ch + 1) * 128, t1:t1 + 514]),
                          reads=zdeps, writes=[bzs2[k]], key=bzs2[k])
                    P.add("sp", lambda h, k=k, ch=ch, t1=t1: h.dma_start(out=bs2[k], in_=bT_d[ch * 128:(ch + 1) * 128, t1:t1 + 512]),
                          reads=[d_b[jj][ch]], writes=[bbs2[k]], key=bbs2[k])
                    a = acc[k]
                    P.add("dve", lambda h, ch=ch, a=a, k=k: h.tensor_scalar(out=a, in0=zs2[k][:, 0:512], scalar1=odw[:, ch * 3:ch * 3 + 1], scalar2=None, op0=ALU.mult),
                          reads=[bzs2[k], bodw], writes=[bacc[k]])
                    for tap in (1, 2):
                        P.add("dve", lambda h, ch=ch, a=a, tap=tap, k=k: h.scalar_tensor_tensor(out=a, in0=zs2[k][:, tap:tap + 512], scalar=odw[:, ch * 3 + tap:ch * 3 + tap + 1], in1=a,
                                                                                              op0=ALU.mult, op1=ALU.add), reads=[bzs2[k], bodw, bacc[k]], writes=[bacc[k]])
                    P.add("pool", lambda h, ch=ch, a=a, k=k: h.tensor_tensor(out=gT[:, ch, :], in0=a, in1=bs2[k], op=ALU.mult), reads=[bacc[k], bbs2[k]], writes=[bgT[ch]])
            if do2:
                jj = j - 1
                t1 = jj * 512
                P.add("sp", lambda h, t1=t1: h.dma_start(out=xt, in_=tile_view(xs_d, t1, 4)), reads=bxs[4 * jj:4 * jj + 4], writes=[bxt], key=bxt)
                out_proj_post(gT, bgT, Wout, bWout, xt, bxt, G_sb, bG, junk, tmp, btmp, b_base=4)
                P.add("sp", lambda h, t1=t1: h.dma_start(out=tile_view(xs_d, t1, 4), in_=xt), reads=[bxt], writes=bxs[4 * jj:4 * jj + 4], key=bxt)
        P.barrier()

    def copy_out():
        AR.reset()
        xt = [AR.alloc([4, D]) for _ in range(2)]
        bxt = [Buf(), Buf()]
        for i in range(nt):
            P.add("sp", lambda h, i=i: h.dma_start(out=xt[i % 2], in_=tile_view(xs_d, i * 512, 4)), reads=bxs[4 * i:4 * i + 4], writes=[bxt[i % 2]], key=bxt[i % 2])
            P.add("sp", lambda h, i=i: h.dma_start(out=tile_view(out_d, i * 512, 4), in_=xt[i % 2]), reads=[bxt[i % 2]], writes=[Buf()], key=bxt[i % 2])
        P.barrier()

    stages = []
    lat_tiles_in, lat_tiles, lat_tiles_out = lat_tiles_in[:min(nt + 1, 8)], lat_tiles[:nt], lat_tiles_out[:nt]
    if stop >= 1:
        ffn(0, 0, ffgu_d[0][0], ffdn_d[0][0], lat_tiles_in + [ctx_tile_in])
    if stop >= 2:
        even_mixer()
    if stop >= 3:
        ffn(0, 2, ffgu_d[1][0], ffdn_d[1][0], lat_tiles)
    if stop >= 4:
        ffn(1, 0, ffgu_d[0][1], ffdn_d[0][1], lat_tiles)
    if stop >= 5:
        odd_mixer()
    if stop >= 6:
        ffn(1, 2, ffgu_d[1][1], ffdn_d[1][1], lat_tiles_out)
    else:
        copy_out()
    P.barrier(engines=("sp",))
    P.emit(nc, stack)
    stack.close()
    return nc


def _host_tables(ev_rpb, ev_dw_w, ev_dw_b, ev_ln_g, ev_ln_b, od_conv_w):
    rpb = np.asarray(ev_rpb[0], np.float32)
    kc = np.arange(64)[:, None]
    qc = np.arange(64)[None, :]
    cs = np.clip(qc - 8, 0, 48)
    valid = (kc >= cs) & (kc < cs + 16)
    cidx = np.clip(kc - qc + 15, 0, 30)
    tbl = np.full((2, 64, 8, 14, 64), NEG, np.float32)
    for par in range(2):
        for e in range(14):
            g = rpb[:, e + par, :][:, cidx]
            g = np.where(valid[None], g, np.float32(NEG))
            tbl[par, :, :, e, :] = np.transpose(g, (1, 0, 2))
    tbl = tbl.reshape(128, 8 * 14 * 64)
    dww = np.asarray(ev_dw_w[0], np.float32)
    dww_l = np.ascontiguousarray(dww.reshape(31, 4, 128).transpose(2, 1, 0)).reshape(128, 124)
    evs = np.stack([np.asarray(a[0], np.float32).reshape(4, 128).T for a in (ev_dw_b, ev_ln_g, ev_ln_b)], axis=1)
    evs = np.ascontiguousarray(evs).reshape(128, 12)
    odw = np.asarray(od_conv_w[0], np.float32)
    odw_l = np.ascontiguousarray(odw.reshape(3, 8, 128).transpose(2, 1, 0)).reshape(128, 24)
    return tbl, dww_l, evs, odw_l


def make_in_maps(x, c, ctx, c_ctx, w_mod, b_mod, norm_g, ff1_w_gu, ff1_w_down, ff2_w_gu, ff2_w_down,
                 ev_w_in, ev_w_out, ev_dw_w, ev_dw_b, ev_ln_g, ev_ln_b, ev_rpb, od_w_in, od_conv_w, od_w_out):
    f = lambda a: np.ascontiguousarray(np.asarray(a, np.float32))
    tbl, dww_l, evs, odw_l = _host_tables(ev_rpb, ev_dw_w, ev_dw_b, ev_ln_g, ev_ln_b, od_conv_w)
    shared = {
        "ident": np.eye(128, dtype=np.float32),
        "w_mod": f(w_mod), "b_mod": f(b_mod), "norm_g": f(norm_g),
        "ff1_w_gu": f(ff1_w_gu), "ff1_w_down": f(ff1_w_down), "ff2_w_gu": f(ff2_w_gu), "ff2_w_down": f(ff2_w_down),
        "ev_w_in": f(ev_w_in[0]), "ev_w_out": f(ev_w_out[0]), "od_w_in": f(od_w_in[0]), "od_w_out": f(od_w_out[0]),
        "tbl": tbl, "dww": dww_l, "evs": evs, "odw": odw_l,
    }
    x = np.asarray(x, np.float32)
    ctx = np.asarray(ctx, np.float32)
    c = np.asarray(c, np.float32)
    c_ctx = np.asarray(c_ctx, np.float32)
    maps = []
    for b in range(8):
        cc = np.stack([c[b].reshape(8, 128).T, c_ctx.reshape(8, 128).T], axis=2)
        m = dict(shared)
        m["x"] = np.ascontiguousarray(x[b])
        m["ctx"] = np.ascontiguousarray(ctx[b])
        m["cc"] = np.ascontiguousarray(cc).reshape(128, 16)
        maps.append(m)
    return maps


_NC_CACHE = {}


def run(inputs, stop=99, trace=False):
    if stop not in _NC_CACHE:
        _NC_CACHE[stop] = build(stop)
    nc = _NC_CACHE[stop]
    maps = make_in_maps(**inputs)
    res = run_bass_kernel_spmd(nc, maps, core_ids=list(range(8)), **({"trace": True} if trace else {}))
    out = np.stack([r["out"] for r in res.results], axis=0)
    return out, res


def kernel(**inputs):
    out, _ = run(inputs)
    return out.astype(np.float32)
```

```python
import contextlib
import numpy as np
import concourse.bass as bass
import concourse.mybir as mybir
from concourse.bass_utils import run_bass_kernel_spmd

F32 = mybir.dt.float32
BF16 = mybir.dt.bfloat16
AF = mybir.ActivationFunctionType
ALU = mybir.AluOpType

D = 1024
SEQ = 4096
CTX = 256
DFF = 2816
NFC = 22
EPS = 1e-6
NEG = -30000.0
ENGS = ("pe", "act", "dve", "pool", "sp")


class Buf:
    __slots__ = ("name", "lw", "rd")

    def __init__(self, name=""):
        self.name = name
        self.lw = None
        self.rd = {}


class Op:
    __slots__ = ("eng", "fn", "deps", "sig", "tok", "dma", "key", "phase")


class Prog:
    def __init__(self):
        self.by_eng = {e: [] for e in ENGS}
        self.ops = []
        self.last = {e: None for e in ENGS}
        self.dmas = []
        self.phase = 0

    def add(self, eng, fn, reads=(), writes=(), key=None):
        op = Op()
        op.eng, op.fn, op.dma, op.key, op.sig, op.tok = eng, fn, key is not None, key, False, None
        op.phase = self.phase
        deps = {}

        def dep(o):
            if o is None or o is op:
                return
            if (not o.dma) and (not op.dma) and o.eng == "pe" and eng == "pe":
                return
            deps[id(o)] = o

        for b in reads:
            dep(b.lw)
        for b in writes:
            dep(b.lw)
            for r in b.rd.values():
                dep(r)
        rk = id(op) if op.dma else eng
        for b in reads:
            b.rd[rk] = op
        for b in writes:
            b.lw = op
            b.rd = {}
        op.deps = list(deps.values())
        self.ops.append(op)
        self.by_eng[eng].append(op)
        if op.dma:
            self.dmas.append(op)
        else:
            self.last[eng] = op
        return op

    def barrier(self, engines=ENGS):
        lastd = {}
        for o in self.dmas:
            lastd[id(o.key)] = o
        deps = [o for o in self.last.values() if o is not None] + list(lastd.values())
        self.dmas = []
        if len(engines) == len(ENGS):
            self.phase += 1
        for e in engines:
            op = Op()
            op.eng, op.fn, op.dma, op.key, op.sig, op.tok = e, None, False, None, False, None
            op.phase = self.phase
            op.deps = [d for d in deps if not (d.eng == "pe" and e == "pe" and not d.dma)]
            self.ops.append(op)
            self.by_eng[e].append(op)

    def emit(self, nc, stack):
        EPOCH = 1000
        for op in self.ops:
            for d in op.deps:
                d.sig = True
        slots, slotd, keymap = [], {}, {}
        for op in self.ops:
            if op.fn is None or not op.dma:
                continue
            km = keymap.setdefault((op.phase, op.eng), {})
            k = id(op.key)
            if k not in km:
                km[k] = len(km)
            sk = (op.eng, km[k])
            if sk not in slotd or (slotd[sk][1] >= 60 and slotd[sk][2] != op.phase):
                slotd[sk] = [stack.enter_context(nc.semaphore(f"d{op.eng}{km[k]}_{len(slotd)}_{op.phase}")), 0, op.phase]
            sl = slotd[sk]
            sl[1] += 1
            op.tok = (sl[0], 16 * sl[1])
        for e in ENGS:
            cnt = 0
            sems = []
            for op in self.by_eng[e]:
                if op.fn is None or op.dma:
                    continue
                if op.sig:
                    ep = cnt // EPOCH
                    if ep >= len(sems):
                        sems.append(stack.enter_context(nc.semaphore(f"e_{e}{ep}")))
                    op.tok = (sems[ep], cnt % EPOCH + 1)
                    cnt += 1
        keysem = slots
        self.nsem = len(keysem)
        by_eng = self.by_eng

        def run(e, h):
            waited = {}
            for op in by_eng[e]:
                for d in op.deps:
                    sem, val = d.tok
                    if waited.get(id(sem), 0) >= val:
                        continue
                    h.wait_ge(sem, val)
                    waited[id(sem)] = val
                if op.fn is None:
                    continue
                ins = op.fn(h)
                if op.dma:
                    ins.then_inc(op.tok[0], 16)
                elif op.sig:
                    ins.then_inc(op.tok[0], 1)

        with nc.Block() as block:
            @block.tensor
            def _(h):
                run("pe", h)

            @block.scalar
            def _(h):
                run("act", h)

            @block.vector
            def _(h):
                run("dve", h)

            @block.gpsimd
            def _(h):
                run("pool", h)

            @block.sync
            def _(h):
                run("sp", h)


class Arena:
    def __init__(self, ap, nwords):
        self.ap, self.n, self.off = ap, nwords, 0

    def reset(self):
        self.off = 0

    def alloc(self, shape, dtype=F32):
        nel = int(np.prod(shape))
        nw = (nel * (4 if dtype == F32 else 2) + 3) // 4
        self.off = (self.off + 7) // 8 * 8
        a = self.ap[:, self.off:self.off + nw]
        self.off += nw
        assert self.off <= self.n, f"arena overflow {self.off} > {self.n}"
        if dtype != F32:
            a = a.bitcast(dtype)
            if a.shape[1] != nel:
                a = a[:, 0:nel]
        if len(shape) == 2:
            a = a.rearrange("p (a b) -> p a b", a=shape[0])
        elif len(shape) == 3:
            a = a.rearrange("p (a b c) -> p a b c", a=shape[0], b=shape[1])
        return a


def build(stop=99, nt=8):
    nc = bass.Bass("TRN2", target_bir_lowering=False)
    P = Prog()
    stack = contextlib.ExitStack()

    def din(name, shape, dt=F32):
        return nc.dram_tensor(name, list(shape), dt, kind="ExternalInput").ap()

    def dscr(name, shape, dt=F32):
        return nc.dram_tensor(name, list(shape), dt, kind="Internal").ap()

    x_d = din("x", [SEQ, D])
    ctx_d = din("ctx", [CTX, D])
    cc_d = din("cc", [128, 16])
    ident_d = din("ident", [128, 128])
    wmod_d = din("w_mod", [2, D, 9 * D])
    bmod_d = din("b_mod", [2, 9 * D])
    ng_d = din("norm_g", [2, 6, D])
    ffgu_d = [din("ff1_w_gu", [2, D, 2 * DFF]), din("ff2_w_gu", [2, D, 2 * DFF])]
    ffdn_d = [din("ff1_w_down", [2, DFF, D]), din("ff2_w_down", [2, DFF, D])]
    evin_d = din("ev_w_in", [D, 2560])
    evout_d = din("ev_w_out", [D, D])
    odin_d = din("od_w_in", [D, 3072])
    odout_d = din("od_w_out", [D, D])
    tbl_d = din("tbl", [128, 8 * 14 * 64])
    dww_d = din("dww", [128, 124])
    evs_d = din("evs", [128, 12])
    odw_d = din("odw", [128, 24])
    out_d = nc.dram_tensor("out", [SEQ, D], F32, kind="ExternalOutput").ap()

    xs_d = dscr("xs", [SEQ + CTX, D])
    modd = dscr("modd", [2, 2, 9, D])
    gluT_d = dscr("gluT", [512, SEQ + 32], BF16)
    qT_d = dscr("qT", [512, SEQ], BF16)
    kT_d = dscr("kT", [512, SEQ + CTX], BF16)
    v_d = dscr("vv", [SEQ + CTX, 512], BF16)
    zT_d = dscr("zT", [D, SEQ + 2])
    bT_d = dscr("bT", [D, SEQ])

    NW = 52000
    arena_t = stack.enter_context(nc.sbuf_tensor("arena", [128, NW], F32))
    AR = Arena(arena_t[:, :], NW)
    cst_t = stack.enter_context(nc.sbuf_tensor("cst", [128, 1024], F32))
    cst = cst_t[:, :]
    ident_f = cst[:, 0:128]
    ident_b = cst[:, 128:192].bitcast(BF16)
    ones_b = cst[:, 192:256].bitcast(BF16)
    stats = cst[:, 256:512]
    NST = 256
    st_bufs = [Buf(f"st{i}") for i in range(NST)]
    st_ctr = [0]

    def stat(n=1):
        i = st_ctr[0]
        if i % NST + n > NST:
            i = (i // NST + 1) * NST
        st_ctr[0] = i + n
        i %= NST
        return stats[:, i:i + n], st_bufs[i]

    ps_t = [stack.enter_context(nc.psum_tensor(f"ps{i}", [128, 512], F32)) for i in range(8)]
    ps = [t[:, :] for t in ps_t]
    bps = [Buf(f"ps{i}") for i in range(8)]
    psT = [ps[b].bitcast(BF16)[:, 0:512] for b in (0, 3, 1, 2)]
    bpsT = [bps[0], bps[3], bps[1], bps[2]]

    b_ident = Buf("ident")
    P.add("sp", lambda h: h.dma_start(out=ident_f, in_=ident_d[:, :]), writes=[b_ident], key=b_ident)
    b_identb = Buf("identb")
    P.add("dve", lambda h: h.tensor_copy(out=ident_b, in_=ident_f), reads=[b_ident], writes=[b_identb])
    P.add("dve", lambda h: h.memset(ones_b, 1.0), writes=[b_identb])
    eps_c = cst[:, 512:513]
    P.add("dve", lambda h: h.memset(eps_c, EPS), writes=[b_identb])

    ctr = {"evac": 0}

    def rstd_of(parts, junk):
        ss, bss = stat(2)
        P.add("dve", lambda h: h.memset(ss, 0.0), writes=[bss])
        for i, (ap, b) in enumerate(parts):
            P.add("act", lambda h, i=i, ap=ap: h.activation(out=junk, in_=ap, func=AF.Square, accum_out=ss[:, i:i + 1]),
                  reads=[b, bss], writes=[bss])
        P.add("dve", lambda h: h.tensor_tensor(out=ss[:, 0:1], in0=ss[:, 0:1], in1=ss[:, 1:2], op=ALU.add), reads=[bss], writes=[bss])
        P.add("act", lambda h: h.activation(out=ss[:, 0:1], in_=ss[:, 0:1], func=AF.Sqrt, scale=1.0 / D, bias=eps_c), reads=[bss, b_identb], writes=[bss])
        P.add("dve", lambda h: h.reciprocal(out=ss[:, 0:1], in_=ss[:, 0:1]), reads=[bss], writes=[bss])
        return ss[:, 0:1], bss

    def prenorm(xget, nsub, A_sb, S_sb, bmod, h_sb, bh, junk, tmp, btmp):
        for s in range(nsub):
            xa, bx = xget(s)
            ss, bss = rstd_of([(xa[:, 0:512], bx), (xa[:, 512:1024], bx)], junk)
            t, bt = tmp[s % 2], btmp[s % 2]
            P.add("dve", lambda h, xa=xa, ss=ss, t=t: h.scalar_tensor_tensor(out=t, in0=xa, scalar=ss, in1=A_sb, op0=ALU.mult, op1=ALU.mult),
                  reads=[bx, bss, bmod[0]], writes=[bt])
            P.add("pool", lambda h, s=s, t=t: h.tensor_tensor(out=h_sb[:, s, :], in0=t, in1=S_sb, op=ALU.add),
                  reads=[bt, bmod[1]], writes=[bh[s]])

    def transposes(h_sb, bh, nsub, hT, bhT):
        for c in range(8):
            k = c % 4
            for s in range(nsub):
                P.add("pe", lambda h, c=c, s=s, k=k: h.transpose(out=psT[k][:, s * 128:(s + 1) * 128], in_=h_sb[:, s, c * 128:(c + 1) * 128], identity=ident_b),
                      reads=[bh[s], b_identb], writes=[bpsT[k]])
            n = nsub * 128
            if c % 2 == 0:
                P.add("act", lambda h, c=c, k=k, n=n: h.copy(out=hT[:, c, 0:n], in_=psT[k][:, 0:n]), reads=[bpsT[k]], writes=[bhT[c]])
            else:
                P.add("dve", lambda h, c=c, k=k, n=n: h.tensor_copy(out=hT[:, c, 0:n], in_=psT[k][:, 0:n]), reads=[bpsT[k]], writes=[bhT[c]])

    def postnorm(banks, G_sb, bG, xs_ap, bxs_, junk, tmp, bt):
        ss, bss = rstd_of([(ps[bk], bps[bk]) for bk in banks], junk)
        for i, bk in enumerate(banks):
            P.add("dve", lambda h, i=i, bk=bk: h.scalar_tensor_tensor(out=tmp[:, i * 512:(i + 1) * 512], in0=ps[bk], scalar=ss,
                                                                   in1=G_sb[:, i * 512:(i + 1) * 512], op0=ALU.mult, op1=ALU.mult),
                  reads=[bps[bk], bss, bG], writes=[bt])
        P.add("pool", lambda h: h.tensor_tensor(out=xs_ap, in0=xs_ap, in1=tmp, op=ALU.add), reads=[bt, bxs_], writes=[bxs_])

    def load_mod(l, v, idx, dst, bdst):
        P.add("sp", lambda h: h.dma_start(out=dst, in_=modd[l, v, idx, :].partition_broadcast(128)), reads=[b_modd], writes=[bdst], key=bdst)

    def tile_view(ap, t0, nsub):
        return ap[t0:t0 + nsub * 128, :].rearrange("(s p) d -> p s d", p=128)

    def load_w_cast(dst, src, bdst):
        P.add("pool", lambda h: h.dma_start(out=dst, in_=src), writes=[bdst], key=bdst)

    b_modd = Buf("modd")
    AR.reset()
    ccs = AR.alloc([16])
    sc = AR.alloc([16])
    wm = [AR.alloc([8, 512]) for _ in range(4)]
    bb = [AR.alloc([512]) for _ in range(4)]
    gl = AR.alloc([6 * D])
    mrow = AR.alloc([9 * D])
    rws = AR.alloc([9 * D])
    b_cc, b_sc, b_gl, b_rows = Buf(), Buf(), Buf(), Buf()
    b_wm, b_bb = [Buf() for _ in range(4)], [Buf() for _ in range(4)]
    b_mrow = [Buf() for _ in range(18)]
    P.add("sp", lambda h: h.dma_start(out=ccs, in_=cc_d[:, :]), writes=[b_cc], key=b_cc)
    P.add("act", lambda h: h.activation(out=sc, in_=ccs, func=AF.Silu), reads=[b_cc], writes=[b_sc])
    scv = sc.rearrange("p (k v) -> p k v", v=2)
    for l in range(2):
        P.add("sp", lambda h, l=l: h.dma_start(out=gl[0:2, :], in_=ng_d[l].rearrange("a d -> (a d)").partition_broadcast(2)),
              writes=[b_gl], key=b_gl)
        wv = wmod_d[l].rearrange("(k p) n -> p k n", p=128)
        for blk in range(18):
            k2 = blk % 4
            P.add("sp", lambda h, wv=wv, blk=blk, k2=k2: h.dma_start(out=wm[k2], in_=wv[:, :, blk * 512:(blk + 1) * 512]),
                  writes=[b_wm[k2]], key=b_wm[k2])
            P.add("sp", lambda h, l=l, blk=blk, k2=k2: h.dma_start(out=bb[k2][0:2, :], in_=bmod_d[l, blk * 512:(blk + 1) * 512].partition_broadcast(2)),
                  writes=[b_bb[k2]], key=b_bb[k2])
            for k in range(8):
                P.add("pe", lambda h, k=k, k2=k2: h.matmul(ps[3][0:2, :], lhsT=scv[:, k, :], rhs=wm[k2][:, k, :], start=(k == 0), stop=(k == 7)),
                      reads=[b_sc, b_wm[k2]], writes=[bps[3]])
            P.add("dve", lambda h, blk=blk, k2=k2: h.tensor_tensor(out=mrow[0:2, blk * 512:(blk + 1) * 512], in0=ps[3][0:2, :], in1=bb[k2][0:2, :], op=ALU.add),
                  reads=[bps[3], b_bb[k2]], writes=[b_mrow[blk]])
        for s in range(3):
            res = 1.0 if s == 1 else 0.5

            def seg(a, i):
                return a[0:2, i * D:(i + 1) * D]
            rb = [b_mrow[2 * (3 * s + i)] for i in range(3)] + [b_mrow[2 * (3 * s + i) + 1] for i in range(3)]
            P.add("dve", lambda h, s=s, seg=seg: h.scalar_tensor_tensor(out=seg(rws, 3 * s), in0=seg(mrow, 3 * s + 1), scalar=1.0, in1=seg(gl, 2 * s),
                                                                     op0=ALU.add, op1=ALU.mult), reads=rb + [b_gl], writes=[b_rows])
            P.add("dve", lambda h, s=s, seg=seg: h.tensor_copy(out=seg(rws, 3 * s + 1), in_=seg(mrow, 3 * s)), reads=rb, writes=[b_rows])
            P.add("dve", lambda h, s=s, seg=seg, res=res: h.scalar_tensor_tensor(out=seg(rws, 3 * s + 2), in0=seg(mrow, 3 * s + 2), scalar=res, in1=seg(gl, 2 * s + 1),
                                                                              op0=ALU.mult, op1=ALU.mult), reads=rb + [b_gl], writes=[b_rows])
        P.add("sp", lambda h, l=l: h.dma_start(out=modd[l].rearrange("v a d -> v (a d)"), in_=rws[0:2, :]), reads=[b_rows], writes=[b_modd], key=b_rows)
    P.barrier()

    def ffn(l, s_idx, wgu_src, wdn_src, tiles):
        AR.reset()
        Wgu = AR.alloc([8, 2 * DFF], BF16)
        Wd = AR.alloc([NFC, D], BF16)
        xr = [AR.alloc([D]) for _ in range(3)]
        hs = AR.alloc([4, D], BF16)
        hT = AR.alloc([8, 512], BF16)
        actT = AR.alloc([NFC, 512], BF16)
        sg0 = AR.alloc([512])
        sg = [sg0, sg0]
        tmp0 = AR.alloc([D])
        tmp = [tmp0, tmp0]
        junk = AR.alloc([512], BF16)
        A_sb, S_sb, G_sb = AR.alloc([D]), AR.alloc([D]), AR.alloc([D])
        bA, bS, bG = Buf("A"), Buf("S"), Buf("G")
        bxr = [Buf() for _ in range(3)]
        xr_ctr = [0]
        bh = [Buf() for _ in range(4)]
        bhT = [Buf() for _ in range(8)]
        bact = [Buf() for _ in range(NFC)]
        bsg0, btmp0 = Buf(), Buf()
        bsg = [bsg0, bsg0]
        btmp = [btmp0, btmp0]
        bWg = [Buf() for _ in range(11)]
        bWu = [Buf() for _ in range(11)]
        bWd = [Buf() for _ in range(11)]
        gv = wgu_src.rearrange("(k p) n -> p k n", p=128)
        dv = wdn_src.rearrange("(j p) d -> p j d", p=128)
        for jj in range(11):
            load_w_cast(Wgu[:, :, jj * 256:(jj + 1) * 256], gv[:, :, jj * 256:(jj + 1) * 256], bWg[jj])
            load_w_cast(Wgu[:, :, DFF + jj * 256:DFF + (jj + 1) * 256], gv[:, :, DFF + jj * 256:DFF + (jj + 1) * 256], bWu[jj])
        for jj in range(11):
            load_w_cast(Wd[:, 2 * jj:2 * jj + 2, :], dv[:, 2 * jj:2 * jj + 2, :], bWd[jj])
        cur_v = [None, None]

        def ensure_mod_pre(v):
            if cur_v[0] != v:
                load_mod(l, v, 3 * s_idx + 0, A_sb, bA)
                load_mod(l, v, 3 * s_idx + 1, S_sb, bS)
                cur_v[0] = v

        def ensure_mod_post(v):
            if cur_v[1] != v:
                load_mod(l, v, 3 * s_idx + 2, G_sb, bG)
                cur_v[1] = v

        def xload(i, s):
            src, dst, nsub, v, sb, db = tiles[i]
            k = xr_ctr[0] % 3
            xr_ctr[0] += 1
            P.add("sp", lambda h: h.dma_start(out=xr[k], in_=src[s * 128:(s + 1) * 128, :]), reads=[sb[s]], writes=[bxr[k]], key=bxr[k])
            return xr[k], bxr[k]

        def pre(i):
            src, dst, nsub, v, sb, db = tiles[i]
            ensure_mod_pre(v)
            prenorm(lambda s: xload(i, s), nsub, A_sb, S_sb, (bA, bS), hs, bh, junk, tmp, btmp)

        n = len(tiles)
        pre(0)
        for i in range(n):
            src, dst, nsub, v, sb, db = tiles[i]
            N = nsub * 128
            if i == 0:
                transposes(hs, bh, nsub, hT, bhT)
            for j in range(NFC):
                for (off, bk, bw) in ((0, 1, bWg[j // 2]), (DFF, 2, bWu[j // 2])):
                    for c in range(8):
                        P.add("pe", lambda h, c=c, j=j, off=off, bk=bk, N=N: h.matmul(ps[bk][:, 0:N], lhsT=Wgu[:, c, off + j * 128:off + (j + 1) * 128],
                                                                                      rhs=hT[:, c, 0:N], start=(c == 0), stop=(c == 7)),
                              reads=[bw, bhT[c]], writes=[bps[bk]])
                k = j % 2
                P.add("act", lambda h, k=k, N=N: h.activation(out=sg[k][:, 0:N], in_=ps[1][:, 0:N], func=AF.Silu), reads=[bps[1]], writes=[bsg[k]])
                P.add("dve", lambda h, k=k, j=j, N=N: h.tensor_tensor(out=actT[:, j, 0:N], in0=ps[2][:, 0:N], in1=sg[k][:, 0:N], op=ALU.mult),
                      reads=[bps[2], bsg[k]], writes=[bact[j]])
                if j == 8 and i + 1 < n:
                    pre(i + 1)
            ensure_mod_post(v)
            for s in range(nsub):
                b0 = 4 + (s % 2) * 2
                for half in range(2):
                    bk = b0 + half
                    for j in range(NFC):
                        P.add("pe", lambda h, j=j, s=s, bk=bk, half=half: h.matmul(ps[bk], lhsT=actT[:, j, s * 128:(s + 1) * 128],
                                                                                   rhs=Wd[:, j, half * 512:(half + 1) * 512], start=(j == 0), stop=(j == NFC - 1)),
                              reads=[bact[j], bWd[j // 2]], writes=[bps[bk]])
                if s == nsub - 1 and i + 1 < n:
                    transposes(hs, bh, tiles[i + 1][2], hT, bhT)
                xa, bxa = xload(i, s)
                postnorm((b0, b0 + 1), G_sb, bG, xa, bxa, junk, tmp[s % 2], btmp[s % 2])
                P.add("sp", lambda h, xa=xa, dst=dst, s=s: h.dma_start(out=dst[s * 128:(s + 1) * 128, :], in_=xa), reads=[bxa], writes=[db[s]], key=bxa)
        P.barrier()

    bxs = [Buf(f"xs{i}") for i in range(34)]

    def rows(ap, t0, n):
        return ap[t0:t0 + n, :]
    lat_tiles_in = [(rows(x_d, i * 512, 512), rows(xs_d, i * 512, 512), 4, 0, [Buf()] * 4, bxs[4 * i:4 * i + 4]) for i in range(8)]
    ctx_tile_in = (rows(ctx_d, 0, 256), rows(xs_d, SEQ, 256), 2, 1, [Buf()] * 2, bxs[32:34])
    lat_tiles = [(rows(xs_d, i * 512, 512), rows(xs_d, i * 512, 512), 4, 0, bxs[4 * i:4 * i + 4], bxs[4 * i:4 * i + 4]) for i in range(8)]
    lat_tiles_out = [(rows(xs_d, i * 512, 512), rows(out_d, i * 512, 512), 4, 0, bxs[4 * i:4 * i + 4], [Buf() for _ in range(4)]) for i in range(8)]

    def mixer_front(l, tiles, AR_bufs):
        pass

    def out_proj_post(yT, byT, Wout, bWout, xt_ap, bxt_, G_sb, bG, junk, tmp, btmp, nsub=4, b_base=2):
        for s in range(nsub):
            b0 = b_base + (s % 2) * 2
            for half in range(2):
                for c in range(8):
                    P.add("pe", lambda h, s=s, c=c, half=half, b0=b0: h.matmul(ps[b0 + half], lhsT=yT[:, c, s * 128:(s + 1) * 128],
                                                                               rhs=Wout[:, c, half * 512:(half + 1) * 512], start=(c == 0), stop=(c == 7)),
                          reads=[byT[c], bWout], writes=[bps[b0 + half]])
            postnorm((b0, b0 + 1), G_sb, bG, xt_ap[:, s, :], bxt_, junk, tmp[s % 2], btmp[s % 2])

    def even_mixer():
        l = 0
        AR.reset()
        Wout = AR.alloc([8, D], BF16)
        A_sb, S_sb, G_sb = AR.alloc([D]), AR.alloc([D]), AR.alloc([D])
        xt0 = AR.alloc([4, D])
        xt = [xt0, xt0]
        hs = AR.alloc([4, D], BF16)
        hT = AR.alloc([8, 512], BF16)
        tmp = [AR.alloc([D]) for _ in range(2)]
        junk = AR.alloc([512], BF16)
        base_off = AR.off
        Win = AR.alloc([8, 2560], BF16)
        bWin = [Buf() for _ in range(5)]
        bWout, btbl, bdiag, bB, bdww, bevs = Buf(), Buf(), Buf(), Buf(), Buf(), Buf()
        bA, bS, bG = Buf(), Buf(), Buf()
        bxt0 = Buf()
        bxt = [bxt0, bxt0]
        bh = [Buf() for _ in range(4)]
        bhT = [Buf() for _ in range(8)]
        btmp = [Buf(), Buf()]
        wv = evin_d.rearrange("(k p) n -> p k n", p=128)
        for g in range(5):
            load_w_cast(Win[:, :, g * 512:(g + 1) * 512], wv[:, :, g * 512:(g + 1) * 512], bWin[g])
        load_w_cast(Wout, evout_d.rearrange("(k p) n -> p k n", p=128), bWout)
        load_mod(0, 0, 3, A_sb, bA)
        load_mod(0, 0, 4, S_sb, bS)
        load_mod(0, 0, 5, G_sb, bG)

        glu_sb = AR.alloc([4, 512], BF16)
        q_sb = AR.alloc([4, 512], BF16)
        k_sb = AR.alloc([4, 512], BF16)
        v_sb = AR.alloc([4, 512], BF16)
        sgm = [AR.alloc([512]) for _ in range(2)]
        zpad = AR.alloc([16], BF16)
        bglu, bq, bk_, bv = Buf(), Buf(), Buf(), Buf()
        bsgm = [Buf(), Buf()]
        bzp = Buf()
        d_glu = [Buf() for _ in range(8)]
        d_q = [Buf() for _ in range(8)]
        d_k = [Buf() for _ in range(9)]
        d_v = [Buf() for _ in range(9)]
        d_pad = Buf()
        gluv = gluT_d.rearrange("(c p) t -> p c t", p=128)
        qv = qT_d.rearrange("(c p) t -> p c t", p=128)
        kv = kT_d.rearrange("(c p) t -> p c t", p=128)
        P.add("dve", lambda h: h.memset(zpad, 0.0), writes=[bzp])
        for c in range(4):
            P.add("sp", lambda h, c=c: h.dma_start(out=gluT_d[c * 128:(c + 1) * 128, 0:16], in_=zpad), reads=[bzp], writes=[d_pad], key=bzp)
            P.add("sp", lambda h, c=c: h.dma_start(out=gluT_d[c * 128:(c + 1) * 128, 15 + SEQ:15 + SEQ + 16], in_=zpad), reads=[bzp], writes=[d_pad], key=bzp)

        def proj(fc, bk, N):
            g = fc // 4
            for c in range(8):
                P.add("pe", lambda h, c=c: h.matmul(ps[bk][:, 0:N], lhsT=Win[:, c, fc * 128:(fc + 1) * 128], rhs=hT[:, c, 0:N], start=(c == 0), stop=(c == 7)),
                      reads=[bWin[g], bhT[c]], writes=[bps[bk]])

        hs2 = AR.alloc([4, D], BF16)
        xr = [AR.alloc([D]) for _ in range(3)]
        bxr = [Buf() for _ in range(3)]
        xr_ctr = [0]
        hsb = [hs, hs2]
        bhb = [bh, [Buf() for _ in range(4)]]
        tiles_e1 = list(range(min(nt + 1, 8))) + [8]

        def xload_e1(j, s):
            k = xr_ctr[0] % 3
            xr_ctr[0] += 1
            P.add("sp", lambda h: h.dma_start(out=xr[k], in_=xs_d[j * 512 + s * 128:j * 512 + (s + 1) * 128, :]), reads=[bxs[4 * j + s]], writes=[bxr[k]], key=bxr[k])
            return xr[k], bxr[k]

        def pre_e1(idx):
            j = tiles_e1[idx]
            nsub = 4 if j < 8 else 2
            if j == 8:
                load_mod(0, 1, 3, A_sb, bA)
                load_mod(0, 1, 4, S_sb, bS)
            prenorm(lambda s: xload_e1(j, s), nsub, A_sb, S_sb, (bA, bS), hsb[idx % 2], bhb[idx % 2], junk, tmp, btmp)

        pre_e1(0)
        for idx, j in enumerate(tiles_e1):
            nsub = 4 if j < 8 else 2
            N = nsub * 128
            t0 = j * 512
            transposes(hsb[idx % 2], bhb[idx % 2], nsub, hT, bhT)
            if idx + 1 < len(tiles_e1):
                pre_e1(idx + 1)
            if j < 8:
                for ch in range(4):
                    k = ch % 2
                    proj(4 + ch, 1, N)
                    P.add("act", lambda h, k=k: h.activation(out=sgm[k], in_=ps[1], func=AF.Sigmoid), reads=[bps[1]], writes=[bsgm[k]])
                    proj(ch, 2, N)
                    P.add("dve", lambda h, k=k, ch=ch: h.tensor_tensor(out=glu_sb[:, ch, :], in0=ps[2], in1=sgm[k], op=ALU.mult),
                          reads=[bps[2], bsgm[k]], writes=[bglu])
                P.add("sp", lambda h, t0=t0: h.dma_start(out=gluv[:, :, 15 + t0:15 + t0 + 512], in_=glu_sb), reads=[bglu, d_pad], writes=[d_glu[j]], key=bglu)
                for ch in range(4):
                    bk = 1 + ch % 2
                    proj(8 + ch, bk, N)
                    P.add("act", lambda h, bk=bk, ch=ch: h.activation(out=q_sb[:, ch, :], in_=ps[bk], func=AF.Copy, scale=0.125), reads=[bps[bk]], writes=[bq])
                P.add("sp", lambda h, t0=t0: h.dma_start(out=qv[:, :, t0:t0 + 512], in_=q_sb), reads=[bq], writes=[d_q[j]], key=bq)
            for ch in range(4):
                bk = 1 + ch % 2
                proj(12 + ch, bk, N)
                P.add("dve", lambda h, bk=bk, ch=ch, N=N: h.tensor_copy(out=k_sb[:, ch, 0:N], in_=ps[bk][:, 0:N]), reads=[bps[bk]], writes=[bk_])
            P.add("sp", lambda h, t0=t0, N=N: h.dma_start(out=kv[:, :, t0:t0 + N], in_=k_sb[:, :, 0:N]), reads=[bk_], writes=[d_k[j]], key=bk_)
            for s in range(nsub):
                bk = 4 + s
                for c in range(8):
                    P.add("pe", lambda h, c=c, s=s, bk=bk: h.matmul(ps[bk], lhsT=hT[:, c, s * 128:(s + 1) * 128], rhs=Win[:, c, 2048:2560], start=(c == 0), stop=(c == 7)),
                          reads=[bWin[4], bhT[c]], writes=[bps[bk]])
                if s % 2 == 0:
                    P.add("act", lambda h, s=s, bk=bk: h.copy(out=v_sb[:, s, :], in_=ps[bk]), reads=[bps[bk]], writes=[bv])
                else:
                    P.add("dve", lambda h, s=s, bk=bk: h.tensor_copy(out=v_sb[:, s, :], in_=ps[bk]), reads=[bps[bk]], writes=[bv])
            P.add("sp", lambda h, t0=t0, nsub=nsub: h.dma_start(out=v_d[t0:t0 + nsub * 128, :].rearrange("(s p) d -> p s d", p=128), in_=v_sb[:, 0:nsub, :]),
                  reads=[bv], writes=[d_v[j]], key=bv)
        P.barrier()
        AR.off = base_off
        tbl = AR.alloc([8, 14, 64])
        diag = AR.alloc([124, 128], BF16)
        Bmat = AR.alloc([128])
        dww = AR.alloc([124])
        evs = AR.alloc([12])
        P.add("sp", lambda h: h.dma_start(out=tbl, in_=tbl_d.rearrange("p (a b c) -> p a b c", a=8, b=14)), writes=[btbl], key=btbl)
        P.add("sp", lambda h: h.dma_start(out=dww, in_=dww_d[:, :]), writes=[bdww], key=bdww)
        P.add("sp", lambda h: h.dma_start(out=evs, in_=evs_d[:, :]), writes=[bevs], key=bevs)
        for idx in range(124):
            P.add("dve", lambda h, idx=idx: h.tensor_scalar(out=diag[:, idx, :], in0=ident_f, scalar1=dww[:, idx:idx + 1], scalar2=None, op0=ALU.mult),
                  reads=[bdww, b_ident], writes=[bdiag])
        P.add("dve", lambda h: h.memset(Bmat, 0.0), writes=[bB])
        P.add("dve", lambda h: h.memset(Bmat[0:64, 0:64], 1.0 / 64), writes=[bB])
        P.add("dve", lambda h: h.memset(Bmat[64:128, 64:128], 1.0 / 64), writes=[bB])
        gw = [AR.alloc([544], BF16) for _ in range(2)]
        cv = AR.alloc([512])
        dd = AR.alloc([512])
        sq = AR.alloc([512])
        rs = AR.alloc([512])
        yab = AR.alloc([8, 512], BF16)
        QTt = AR.alloc([4, 512], BF16)
        KTw = AR.alloc([4, 1024], BF16)
        Vw = AR.alloc([8, 512], BF16)
        KTc = AR.alloc([4, 256], BF16)
        Vc = AR.alloc([2, 512], BF16)
        sbA = [AR.alloc([512]) for _ in range(2)]
        sbB = [AR.alloc([128]) for _ in range(2)]
        PT = [AR.alloc([7, 8, 64], BF16) for _ in range(2)]
        Qbd = AR.alloc([4, 8, 128], BF16)
        bQbd = Buf()
        P.add("pool", lambda h: h.memset(Qbd, 0.0), writes=[bQbd])
        rden = AR.alloc([512])
        bgw = [Buf(), Buf()]
        bcv, bdd, bsq, brs = Buf(), Buf(), Buf(), Buf()
        byab = [Buf() for _ in range(8)]
        bQ, bK, bV, bKc, bVc = Buf(), Buf(), Buf(), Buf(), Buf()
        bsbA, bsbB = [Buf(), Buf()], [Buf(), Buf()]
        bPT = [[Buf() for _ in range(4)] for _ in range(2)]
        brden = Buf()
        P.add("sp", lambda h: h.dma_start(out=KTc, in_=kv[:, :, SEQ:SEQ + CTX]), reads=[d_k[8]], writes=[bKc], key=bKc)
        P.add("sp", lambda h: h.dma_start(out=Vc, in_=v_d[SEQ:SEQ + CTX, :].rearrange("(s p) d -> p s d", p=128)), reads=[d_v[8]], writes=[bVc], key=bVc)
        tblv = tbl

        def e3_loads(j):
            t0 = j * 512
            win_lo = min(max(8 * j - 4, 0), 48)
            P.add("sp", lambda h, t0=t0: h.dma_start(out=QTt, in_=qv[:, :, t0:t0 + 512]), reads=[d_q[j]], writes=[bQ], key=bQ)
            for par in range(2):
                p0 = par * 64
                P.add("pool", lambda h, p0=p0: h.tensor_copy(out=Qbd[p0:p0 + 64, :, :, p0:p0 + 64], in_=QTt[p0:p0 + 64, :, :].rearrange("p c (r q) -> p c r q", q=64)),
                      reads=[bQ], writes=[bQbd])
            wt0 = win_lo * 64
            wtiles = sorted(set([wt0 // 512, (wt0 + 1023) // 512]))
            P.add("sp", lambda h, wt0=wt0: h.dma_start(out=KTw, in_=kv[:, :, wt0:wt0 + 1024]), reads=[d_k[t] for t in wtiles], writes=[bK], key=bK)
            P.add("sp", lambda h, wt0=wt0: h.dma_start(out=Vw, in_=v_d[wt0:wt0 + 1024, :].rearrange("(s p) d -> p s d", p=128)),
                  reads=[d_v[t] for t in wtiles], writes=[bV], key=bV)

        for j in range(nt):
            t0 = j * 512
            win_lo = min(max(8 * j - 4, 0), 48)
            def conv(ch):
                k = ch % 2
                P.add("sp", lambda h, ch=ch, k=k, t0=t0: h.dma_start(out=gw[k][:, 0:542], in_=gluT_d[ch * 128:(ch + 1) * 128, t0:t0 + 542]),
                      reads=[d_glu[max(j - 1, 0)], d_glu[j], d_glu[min(j + 1, 7)], d_pad], writes=[bgw[k]], key=bgw[k])
                for tap in range(31):
                    P.add("pe", lambda h, ch=ch, k=k, tap=tap: h.matmul(ps[k], lhsT=diag[:, ch * 31 + tap, :], rhs=gw[k][:, tap:tap + 512], start=(tap == 0), stop=(tap == 30)),
                          reads=[bdiag, bgw[k]], writes=[bps[k]])

            conv(0)
            for ch in range(4):
                k = ch % 2
                if ch + 1 < 4:
                    conv(ch + 1)
                P.add("act", lambda h, ch=ch, k=k: h.activation(out=cv, in_=ps[k], func=AF.Identity, bias=evs[:, ch:ch + 1]), reads=[bps[k], bevs], writes=[bcv])
                P.add("pe", lambda h: h.matmul(ps[6], lhsT=Bmat, rhs=cv, start=True, stop=True), reads=[bB, bcv], writes=[bps[6]])
                P.add("dve", lambda h: h.tensor_tensor(out=dd, in0=cv, in1=ps[6], op=ALU.subtract), reads=[bcv, bps[6]], writes=[bdd])
                P.add("act", lambda h: h.activation(out=sq, in_=dd, func=AF.Square), reads=[bdd], writes=[bsq])
                P.add("pe", lambda h: h.matmul(ps[7], lhsT=Bmat, rhs=sq, start=True, stop=True), reads=[bB, bsq], writes=[bps[7]])
                P.add("act", lambda h: h.activation(out=rs, in_=ps[7], func=AF.Sqrt, bias=eps_c), reads=[bps[7]], writes=[brs])
                P.add("dve", lambda h: h.reciprocal(out=rs, in_=rs), reads=[brs], writes=[brs])
                P.add("dve", lambda h: h.tensor_tensor(out=dd, in0=dd, in1=rs, op=ALU.mult), reads=[bdd, brs], writes=[bdd])
                P.add("act", lambda h, ch=ch: h.activation(out=yab[:, ch, :], in_=dd, func=AF.Silu, scale=evs[:, 4 + ch:5 + ch], bias=evs[:, 8 + ch:9 + ch]),
                      reads=[bdd, bevs], writes=[byab[ch]])
            if j == 0:
                e3_loads(0)
            def scores(rr):
                r = 8 * j + rr
                rs_ = min(max(r - 4, 0), 56)
                if rs_ % 2 == 0:
                    starts = [rs_ + 2 * t for t in range(4)]
                    kr = [(0, 128)] * 4
                else:
                    starts = [rs_ - 1 + 2 * t for t in range(5)]
                    kr = [(64, 128), (0, 128), (0, 128), (0, 128), (0, 64)]
                ntl = len(starts)
                e0 = starts[0] - r + 7
                assert 0 <= e0 and e0 + 2 * (ntl - 1) <= 13
                pb = r % 2
                n4 = min(ntl, 4)
                for g in range(4):
                    kk3 = (rr * 4 + g) % 3
                    bA_, bB_ = 2 * kk3, 2 * kk3 + 1
                    k2 = g % 2
                    for t in range(ntl):
                        ko = (starts[t] - win_lo) * 64
                        dstb, dlo = (bA_, t * 128) if t < 4 else (bB_, 0)
                        P.add("pe", lambda h, ko=ko, g=g, dstb=dstb, dlo=dlo, rr=rr: h.matmul(ps[dstb][:, dlo:dlo + 128], lhsT=KTw[:, g, ko:ko + 128],
                                                                                              rhs=Qbd[:, g, rr, :], start=True, stop=True),
                              reads=[bK, bQbd], writes=[bps[dstb]])
                    for cx in range(2):
                        P.add("pe", lambda h, cx=cx, g=g, bB_=bB_, rr=rr: h.matmul(ps[bB_][:, (1 + cx) * 128:(2 + cx) * 128], lhsT=KTc[:, g, cx * 128:(cx + 1) * 128],
                                                                                   rhs=Qbd[:, g, rr, :], start=True, stop=True),
                              reads=[bKc, bQbd], writes=[bps[bB_]])
                    P.add("dve", lambda h, g=g, k2=k2, n4=n4, e0=e0, bA_=bA_: h.tensor_tensor(
                        out=sbA[k2][:, 0:n4 * 128].rearrange("p (a b c) -> p a b c", b=2, c=64),
                        in0=ps[bA_][:, 0:n4 * 128].rearrange("p (a b c) -> p a b c", b=2, c=64),
                        in1=tblv[:, 2 * g:2 * g + 2, e0:e0 + 2 * n4 - 1:2, :].transpose([0, 2, 1, 3]), op=ALU.add),
                        reads=[bps[bA_], btbl], writes=[bsbA[k2]])
                    P.add("act", lambda h, g=g, k2=k2, n4=n4, pb=pb: h.activation(out=PT[pb][:, 0:n4, 2 * g:2 * g + 2, :],
                                                                                 in_=sbA[k2][:, 0:n4 * 128].rearrange("p (a b c) -> p a b c", b=2, c=64), func=AF.Exp),
                          reads=[bsbA[k2]], writes=[bPT[pb][g]])
                    if ntl == 5:
                        P.add("dve", lambda h, g=g, k2=k2, e0=e0, bB_=bB_: h.tensor_tensor(
                            out=sbB[k2].rearrange("p (b c) -> p b c", c=64), in0=ps[bB_][:, 0:128].rearrange("p (b c) -> p b c", c=64),
                            in1=tblv[:, 2 * g:2 * g + 2, e0 + 8, :], op=ALU.add), reads=[bps[bB_], btbl], writes=[bsbB[k2]])
                        P.add("act", lambda h, g=g, k2=k2, pb=pb: h.activation(out=PT[pb][:, 4, 2 * g:2 * g + 2, :], in_=sbB[k2].rearrange("p (b c) -> p b c", c=64), func=AF.Exp),
                              reads=[bsbB[k2]], writes=[bPT[pb][g]])
                    P.add("act", lambda h, g=g, pb=pb, bB_=bB_: h.activation(out=PT[pb][:, 5:7, 2 * g:2 * g + 2, :],
                                                                             in_=ps[bB_][:, 128:384].rearrange("p (a b c) -> p a b c", b=2, c=64), func=AF.Exp),
                          reads=[bps[bB_]], writes=[bPT[pb][g]])
            def pvpart(rr):
                r = 8 * j + rr
                rs_ = min(max(r - 4, 0), 56)
                if rs_ % 2 == 0:
                    starts = [rs_ + 2 * t for t in range(4)]
                    kr = [(0, 128)] * 4
                else:
                    starts = [rs_ - 1 + 2 * t for t in range(5)]
                    kr = [(64, 128), (0, 128), (0, 128), (0, 128), (0, 64)]
                ntl = len(starts)
                e0 = starts[0] - r + 7
                assert 0 <= e0 and e0 + 2 * (ntl - 1) <= 13
                pb = r % 2
                items = [(kr[t], ("w", (starts[t] - win_lo) // 2), t) for t in range(ntl)] + [((0, 128), ("c", 0), 5), ((0, 128), ("c", 1), 6)]
                for g in range(4):
                    for ii, ((p0, p1), (kind, vi), t) in enumerate(items):
                        vsrc, bvs = (Vw, bV) if kind == "w" else (Vc, bVc)
                        P.add("pe", lambda h, g=g, p0=p0, p1=p1, vsrc=vsrc, vi=vi, t=t, ii=ii, pb=pb, nit=len(items): h.matmul(
                            ps[6][:, g * 128:(g + 1) * 128], lhsT=vsrc[p0:p1, vi, g * 128:(g + 1) * 128], rhs=PT[pb][p0:p1, t, 2 * g:2 * g + 2, :],
                            start=(ii == 0), stop=(ii == nit - 1)), reads=[bvs, bPT[pb][g]], writes=[bps[6]])
                for ii, ((p0, p1), _, t) in enumerate(items):
                    P.add("pe", lambda h, p0=p0, p1=p1, t=t, ii=ii, pb=pb, nit=len(items): h.matmul(ps[7], lhsT=ones_b[p0:p1, :], rhs=PT[pb][p0:p1, t, :, :],
                                                                                    start=(ii == 0), stop=(ii == nit - 1)),
                          reads=[b_identb] + bPT[pb], writes=[bps[7]])
                P.add("dve", lambda h: h.reciprocal(out=rden, in_=ps[7]), reads=[bps[7]], writes=[brden])
                for par in range(2):
                    p0 = par * 64
                    P.add("dve", lambda h, par=par, p0=p0, rr=rr: h.tensor_tensor(
                        out=yab[p0:p0 + 64, 4:8, rr * 64:(rr + 1) * 64],
                        in0=ps[6][p0:p0 + 64, :].rearrange("p (g two q) -> p g two q", two=2, q=64)[:, :, par, :],
                        in1=rden[p0:p0 + 64, :].rearrange("p (g two q) -> p g two q", two=2, q=64)[:, :, par, :], op=ALU.mult),
                        reads=[bps[6], brden], writes=byab[4:8])
            scores(0)
            for rr in range(8):
                if rr + 1 < 8:
                    scores(rr + 1)
                pvpart(rr)
            if j + 1 < nt:
                e3_loads(j + 1)
            P.add("sp", lambda h, j=j, t0=t0: h.dma_start(out=xt[j % 2], in_=tile_view(xs_d, t0, 4)), reads=bxs[4 * j:4 * j + 4], writes=[bxt[j % 2]], key=bxt[j % 2])
            out_proj_post(yab, byab, Wout, bWout, xt[j % 2], bxt[j % 2], G_sb, bG, junk, tmp, btmp)
            P.add("sp", lambda h, j=j, t0=t0: h.dma_start(out=tile_view(xs_d, t0, 4), in_=xt[j % 2]), reads=[bxt[j % 2]], writes=bxs[4 * j:4 * j + 4], key=bxt[j % 2])
        P.barrier()

    def odd_mixer():
        l = 1
        AR.reset()
        Win = AR.alloc([8, 3072], BF16)
        Wout = AR.alloc([8, D], BF16)
        odw = AR.alloc([24])
        A_sb, S_sb, G_sb = AR.alloc([D]), AR.alloc([D]), AR.alloc([D])
        xr = [AR.alloc([D]) for _ in range(3)]
        hs = [AR.alloc([4, D], BF16) for _ in range(2)]
        hT = AR.alloc([8, 512], BF16)
        tmp = [AR.alloc([D]) for _ in range(2)]
        junk = AR.alloc([512], BF16)
        zs1 = [AR.alloc([512]) for _ in range(2)]
        bs1 = [AR.alloc([512]) for _ in range(2)]
        csb = [AR.alloc([512]) for _ in range(2)]
        zs2 = [AR.alloc([520]) for _ in range(2)]
        bs2 = [AR.alloc([512]) for _ in range(2)]
        acc = [AR.alloc([512]) for _ in range(2)]
        gT = AR.alloc([8, 512], BF16)
        xt = AR.alloc([4, D])
        zp = AR.alloc([8])
        bWin = [Buf() for _ in range(6)]
        bWout, bodw = Buf(), Buf()
        bA, bS, bG = Buf(), Buf(), Buf()
        bxr = [Buf() for _ in range(3)]
        xr_ctr = [0]
        bh = [[Buf() for _ in range(4)] for _ in range(2)]
        bhT = [Buf() for _ in range(8)]
        btmp = [Buf(), Buf()]
        bzs1, bbs1, bcsb, bzs2, bbs2, bacc = ([Buf(), Buf()] for _ in range(6))
        bgT = [Buf() for _ in range(8)]
        bxt, bzp = Buf(), Buf()
        d_z = [[Buf() for _ in range(8)] for _ in range(8)]
        d_b = [[Buf() for _ in range(8)] for _ in range(8)]
        d_pad = Buf()
        wv = odin_d.rearrange("(k p) n -> p k n", p=128)
        for g in (2, 4, 0, 3, 5, 1):
            load_w_cast(Win[:, :, g * 512:(g + 1) * 512], wv[:, :, g * 512:(g + 1) * 512], bWin[g])
        load_w_cast(Wout, odout_d.rearrange("(k p) n -> p k n", p=128), bWout)
        P.add("sp", lambda h: h.dma_start(out=odw, in_=odw_d[:, :]), writes=[bodw], key=bodw)
        load_mod(1, 0, 3, A_sb, bA)
        load_mod(1, 0, 4, S_sb, bS)
        load_mod(1, 0, 5, G_sb, bG)
        zv = zT_d.rearrange("(c p) t -> p c t", p=128)
        P.add("dve", lambda h: h.memset(zp, 0.0), writes=[bzp])
        P.add("sp", lambda h: h.dma_start(out=zv[:, :, 0:1], in_=zp.rearrange("p (a b) -> p a b", b=1), allow_slow_non_contiguous=True), reads=[bzp], writes=[d_pad], key=bzp)
        P.add("sp", lambda h: h.dma_start(out=zv[:, :, SEQ + 1:SEQ + 2], in_=zp.rearrange("p (a b) -> p a b", b=1), allow_slow_non_contiguous=True), reads=[bzp], writes=[d_pad], key=bzp)

        def proj(fc, bk):
            g = fc // 4
            for c in range(8):
                P.add("pe", lambda h, c=c: h.matmul(ps[bk], lhsT=Win[:, c, fc * 128:(fc + 1) * 128], rhs=hT[:, c, :], start=(c == 0), stop=(c == 7)),
                      reads=[bWin[g], bhT[c]], writes=[bps[bk]])

        def xload(j, s):
            k = xr_ctr[0] % 3
            xr_ctr[0] += 1
            P.add("sp", lambda h: h.dma_start(out=xr[k], in_=xs_d[j * 512 + s * 128:j * 512 + (s + 1) * 128, :]), reads=[bxs[4 * j + s]], writes=[bxr[k]], key=bxr[k])
            return xr[k], bxr[k]

        def pre(j):
            prenorm(lambda s: xload(j, s), 4, A_sb, S_sb, (bA, bS), hs[j % 2], bh[j % 2], junk, tmp, btmp)

        NT1 = min(nt + 1, 8)
        pre(0)
        for j in range(NT1 + 1):
            do1 = j < NT1
            do2 = 1 <= j <= nt
            if do1 and j == 0:
                transposes(hs[0], bh[0], 4, hT, bhT)
                if 1 < NT1:
                    pre(1)
            for chh in range(9):
                ch = chh
                k = ch % 2
                if do1 and chh < 8:
                    t0 = j * 512
                    proj(8 + ch, 1)
                    P.add("act", lambda h, k=k: h.copy(out=csb[k], in_=ps[1]), reads=[bps[1]], writes=[bcsb[k]])
                    proj(16 + ch, 2)
                    P.add("dve", lambda h, k=k: h.tensor_tensor(out=zs1[k], in0=ps[2], in1=csb[k], op=ALU.mult), reads=[bps[2], bcsb[k]], writes=[bzs1[k]])
                    P.add("sp", lambda h, k=k, ch=ch, t0=t0: h.dma_start(out=zT_d[ch * 128:(ch + 1) * 128, 1 + t0:1 + t0 + 512], in_=zs1[k]),
                          reads=[bzs1[k], d_pad], writes=[d_z[j][ch]], key=bzs1[k])
                    proj(ch, 3)
                    P.add("act", lambda h, k=k: h.copy(out=bs1[k], in_=ps[3]), reads=[bps[3]], writes=[bbs1[k]])
                    P.add("sp", lambda h, k=k, ch=ch, t0=t0: h.dma_start(out=bT_d[ch * 128:(ch + 1) * 128, t0:t0 + 512], in_=bs1[k]),
                          reads=[bbs1[k]], writes=[d_b[j][ch]], key=bbs1[k])
                if do2 and chh >= 1:
                    ch = chh - 1
                    k = ch % 2
                    jj = j - 1
                    t1 = jj * 512
                    zdeps = [d_z[max(jj - 1, 0)][ch], d_z[jj][ch], d_z[min(jj + 1, 7)][ch], d_pad]
                    P.add("sp", lambda h, k=k, ch=ch, t1=t1: h.dma_start(out=zs2[k][:, 0:514], in_=zT_d[ch * 128:(ch + 1) * 128, t1:t1 + 514]),
                          reads=zdeps, writes=[bzs2[k]], key=bzs2[k])
                    P.add("sp", lambda h, k=k, ch=ch, t1=t1: h.dma_start(out=bs2[k], in_=bT_d[ch * 128:(ch + 1) * 128, t1:t1 + 512]),
                          reads=[d_b[jj][ch]], writes=[bbs2[k]], key=bbs2[k])
                    a = acc[k]
                    P.add("dve", lambda h, ch=ch, a=a, k=k: h.tensor_scalar(out=a, in0=zs2[k][:, 0:512], scalar1=odw[:, ch * 3:ch * 3 + 1], scalar2=None, op0=ALU.mult),
                          reads=[bzs2[k], bodw], writes=[bacc[k]])
                    for tap in (1, 2):
                        P.add("dve", lambda h, ch=ch, a=a, tap=tap, k=k: h.scalar_tensor_tensor(out=a, in0=zs2[k][:, tap:tap + 512], scalar=odw[:, ch * 3 + tap:ch * 3 + tap + 1], in1=a,
                                                                                              op0=ALU.mult, op1=ALU.add), reads=[bzs2[k], bodw, bacc[k]], writes=[bacc[k]])
                    P.add("pool", lambda h, ch=ch, a=a, k=k: h.tensor_tensor(out=gT[:, ch, :], in0=a, in1=bs2[k], op=ALU.mult), reads=[bacc[k], bbs2[k]], writes=[bgT[ch]])
            if j + 1 < NT1:
                transposes(hs[(j + 1) % 2], bh[(j + 1) % 2], 4, hT, bhT)
                if j + 2 < NT1:
                    pre(j + 2)
            if do2:
                jj = j - 1
                t1 = jj * 512
                P.add("sp", lambda h, t1=t1: h.dma_start(out=xt, in_=tile_view(xs_d, t1, 4)), reads=bxs[4 * jj:4 * jj + 4], writes=[bxt], key=bxt)
                out_proj_post(gT, bgT, Wout, bWout, xt, bxt, G_sb, bG, junk, tmp, btmp, b_base=4)
                P.add("sp", lambda h, t1=t1: h.dma_start(out=tile_view(xs_d, t1, 4), in_=xt), reads=[bxt], writes=bxs[4 * jj:4 * jj + 4], key=bxt)
        P.barrier()

    def copy_out():
        AR.reset()
        xt = [AR.alloc([4, D]) for _ in range(2)]
        bxt = [Buf(), Buf()]
        for i in range(nt):
            P.add("sp", lambda h, i=i: h.dma_start(out=xt[i % 2], in_=tile_view(xs_d, i * 512, 4)), reads=bxs[4 * i:4 * i + 4], writes=[bxt[i % 2]], key=bxt[i % 2])
            P.add("sp", lambda h, i=i: h.dma_start(out=tile_view(out_d, i * 512, 4), in_=xt[i % 2]), reads=[bxt[i % 2]], writes=[Buf()], key=bxt[i % 2])
        P.barrier()

    stages = []
    lat_tiles_in, lat_tiles, lat_tiles_out = lat_tiles_in[:min(nt + 1, 8)], lat_tiles[:nt], lat_tiles_out[:nt]
    if stop >= 1:
        ffn(0, 0, ffgu_d[0][0], ffdn_d[0][0], lat_tiles_in + [ctx_tile_in])
    if stop >= 2:
        even_mixer()
    if stop >= 3:
        ffn(0, 2, ffgu_d[1][0], ffdn_d[1][0], lat_tiles)
    if stop >= 4:
        ffn(1, 0, ffgu_d[0][1], ffdn_d[0][1], lat_tiles)
    if stop >= 5:
        odd_mixer()
    if stop >= 6:
        ffn(1, 2, ffgu_d[1][1], ffdn_d[1][1], lat_tiles_out)
    else:
        copy_out()
    P.barrier(engines=("sp",))
    P.emit(nc, stack)
    stack.close()
    return nc


def _host_tables(ev_rpb, ev_dw_w, ev_dw_b, ev_ln_g, ev_ln_b, od_conv_w):
    rpb = np.asarray(ev_rpb[0], np.float32)
    kc = np.arange(64)[:, None]
    qc = np.arange(64)[None, :]
    cs = np.clip(qc - 8, 0, 48)
    valid = (kc >= cs) & (kc < cs + 16)
    cidx = np.clip(kc - qc + 15, 0, 30)
    tbl = np.full((2, 64, 8, 14, 64), NEG, np.float32)
    for par in range(2):
        for e in range(14):
            g = rpb[:, e + par, :][:, cidx]
            g = np.where(valid[None], g, np.float32(NEG))
            tbl[par, :, :, e, :] = np.transpose(g, (1, 0, 2))
    tbl = tbl.reshape(128, 8 * 14 * 64)
    dww = np.asarray(ev_dw_w[0], np.float32)
    dww_l = np.ascontiguousarray(dww.reshape(31, 4, 128).transpose(2, 1, 0)).reshape(128, 124)
    evs = np.stack([np.asarray(a[0], np.float32).reshape(4, 128).T for a in (ev_dw_b, ev_ln_g, ev_ln_b)], axis=1)
    evs = np.ascontiguousarray(evs).reshape(128, 12)
    odw = np.asarray(od_conv_w[0], np.float32)
    odw_l = np.ascontiguousarray(odw.reshape(3, 8, 128).transpose(2, 1, 0)).reshape(128, 24)
    return tbl, dww_l, evs, odw_l


def make_in_maps(x, c, ctx, c_ctx, w_mod, b_mod, norm_g, ff1_w_gu, ff1_w_down, ff2_w_gu, ff2_w_down,
                 ev_w_in, ev_w_out, ev_dw_w, ev_dw_b, ev_ln_g, ev_ln_b, ev_rpb, od_w_in, od_conv_w, od_w_out):
    f = lambda a: np.ascontiguousarray(np.asarray(a, np.float32))
    tbl, dww_l, evs, odw_l = _host_tables(ev_rpb, ev_dw_w, ev_dw_b, ev_ln_g, ev_ln_b, od_conv_w)
    shared = {
        "ident": np.eye(128, dtype=np.float32),
        "w_mod": f(w_mod), "b_mod": f(b_mod), "norm_g": f(norm_g),
        "ff1_w_gu": f(ff1_w_gu), "ff1_w_down": f(ff1_w_down), "ff2_w_gu": f(ff2_w_gu), "ff2_w_down": f(ff2_w_down),
        "ev_w_in": f(ev_w_in[0]), "ev_w_out": f(ev_w_out[0]), "od_w_in": f(od_w_in[0]), "od_w_out": f(od_w_out[0]),
        "tbl": tbl, "dww": dww_l, "evs": evs, "odw": odw_l,
    }
    x = np.asarray(x, np.float32)
    ctx = np.asarray(ctx, np.float32)
    c = np.asarray(c, np.float32)
    c_ctx = np.asarray(c_ctx, np.float32)
    maps = []
    for b in range(8):
        cc = np.stack([c[b].reshape(8, 128).T, c_ctx.reshape(8, 128).T], axis=2)
        m = dict(shared)
        m["x"] = np.ascontiguousarray(x[b])
        m["ctx"] = np.ascontiguousarray(ctx[b])
        m["cc"] = np.ascontiguousarray(cc).reshape(128, 16)
        maps.append(m)
    return maps


_NC_CACHE = {}


def run(inputs, stop=99, trace=False):
    if stop not in _NC_CACHE:
        _NC_CACHE[stop] = build(stop)
    nc = _NC_CACHE[stop]
    maps = make_in_maps(**inputs)
    res = run_bass_kernel_spmd(nc, maps, core_ids=list(range(8)), **({"trace": True} if trace else {}))
    out = np.stack([r["out"] for r in res.results], axis=0)
    return out, res


def kernel(**inputs):
    out, _ = run(inputs)
    return out.astype(np.float32)
```
